# Optimizing a Trainium2 kernel written in Bass

```python
import jax, jax.numpy as jnp
from jax import lax
import numpy as np

D_MODEL = 2048
BATCH = 4
SEQ = 4096
DEPTH = 1

GRID_W = 64
MEM_LEN = 256
Q_BLOCK = 128
ROPE_THETA = 10000.0
EPS = 1e-6
MLA_HEADS = 8
MLA_Q_RANK = 512
MLA_KV_RANK = 512
MLA_NOPE = 128
MLA_ROPE = 64
MLA_V = 128
GQA_HEADS = 8
GQA_KV_HEADS = 2
GQA_HEAD_DIM = 128
X_HEADS = 4
X_HEAD_DIM = 128
D_FF = 256 * ((8 * D_MODEL // 3 + 255) // 256)
CONV_W = 3

MIX_A_WIDTH = MLA_HEADS * MLA_V
MIX_B_WIDTH = GQA_HEADS * GQA_HEAD_DIM
IN_SIZES = (MLA_Q_RANK, MLA_KV_RANK, MLA_ROPE,
            GQA_HEADS * GQA_HEAD_DIM, GQA_KV_HEADS * GQA_HEAD_DIM, GQA_KV_HEADS * GQA_HEAD_DIM)
N_IN = sum(IN_SIZES)
IN_SPLITS = [int(v) for v in np.cumsum(IN_SIZES)[:-1]]

kernel_name = 'hybrid_mla_gqa2d_gated_encoder_block'


def rms_norm(x, g):
    xf = x.astype(jnp.float32)
    y = xf * lax.rsqrt(jnp.mean(xf * xf, axis=-1, keepdims=True) + EPS)
    return (y * g.astype(jnp.float32)).astype(x.dtype)


def rope(x, pos):
    d = x.shape[-1]
    half = d // 2
    inv = jnp.power(ROPE_THETA, -2.0 * jnp.arange(half, dtype=jnp.float32) / d)
    ang = pos[:, None] * inv[None, :]
    cos = jnp.cos(ang)[None, :, None, :]
    sin = jnp.sin(ang)[None, :, None, :]
    xf = x.astype(jnp.float32)
    x1, x2 = xf[..., :half], xf[..., half:]
    return jnp.concatenate([x1 * cos - x2 * sin, x1 * sin + x2 * cos], axis=-1).astype(x.dtype)


def rope_2d(x, rows, cols):
    half = x.shape[-1] // 2
    return jnp.concatenate([rope(x[..., :half], rows), rope(x[..., half:], cols)], axis=-1)


def blocked_attention(q, k, v, scale):
    B, L, H, Dk = q.shape
    G = k.shape[2]
    R = H // G
    nb = L // Q_BLOCK
    qb = q.reshape(B, nb, Q_BLOCK, G, R, Dk).transpose(1, 0, 2, 3, 4, 5)

    def one_block(q_blk):
        s = jnp.einsum('bqgrd,bkgd->bgrqk', q_blk, k).astype(jnp.float32) * scale
        p = jax.nn.softmax(s, axis=-1).astype(v.dtype)
        return jnp.einsum('bgrqk,bkgd->bqgrd', p, v)

    out = lax.map(one_block, qb)
    return out.transpose(1, 0, 2, 3, 4, 5).reshape(B, L, H, v.shape[-1])


def mla_branch(c_q, c_kv, k_r, q_norm, w_uq, kv_norm, w_ukv, tpos):
    B, L, _ = c_q.shape
    q = (rms_norm(c_q, q_norm) @ w_uq).reshape(B, L, MLA_HEADS, MLA_NOPE + MLA_ROPE)
    q = jnp.concatenate([q[..., :MLA_NOPE], rope(q[..., MLA_NOPE:], tpos)], axis=-1)
    kv = (rms_norm(c_kv, kv_norm) @ w_ukv).reshape(B, L, MLA_HEADS, MLA_NOPE + MLA_V)
    k_nope, v = kv[..., :MLA_NOPE], kv[..., MLA_NOPE:]
    k_rope = rope(k_r[:, :, None, :], tpos)
    k = jnp.concatenate([k_nope, jnp.broadcast_to(k_rope, (B, L, MLA_HEADS, MLA_ROPE))], axis=-1)
    o = blocked_attention(q, k, v, (MLA_NOPE + MLA_ROPE) ** -0.5)
    return o.reshape(B, L, MIX_A_WIDTH)


def gqa_branch(qB, kB, vB, q_gain, k_gain, rows, cols):
    B, L, _ = qB.shape
    q = rms_norm(qB.reshape(B, L, GQA_HEADS, GQA_HEAD_DIM), q_gain)
    k = rms_norm(kB.reshape(B, L, GQA_KV_HEADS, GQA_HEAD_DIM), k_gain)
    v = vB.reshape(B, L, GQA_KV_HEADS, GQA_HEAD_DIM)
    q = rope_2d(q, rows, cols)
    k = rope_2d(k, rows, cols)
    o = blocked_attention(q, k, v, GQA_HEAD_DIM ** -0.5)
    return o.reshape(B, L, MIX_B_WIDTH)


def memory_cross_attention(hc, mem_n, w_xq, w_xkv, w_xo):
    B, L, _ = hc.shape
    M = mem_n.shape[1]
    q = (hc @ w_xq).reshape(B, L, X_HEADS, X_HEAD_DIM)
    kv = (mem_n @ w_xkv).reshape(B, M, X_HEADS, 2 * X_HEAD_DIM)
    k, v = kv[..., :X_HEAD_DIM], kv[..., X_HEAD_DIM:]
    o = blocked_attention(q, k, v, X_HEAD_DIM ** -0.5)
    return o.reshape(B, L, X_HEADS * X_HEAD_DIM) @ w_xo


def conv_glu_ffn(h, w_up, conv_w, conv_b, w_down):
    u = h @ w_up
    C = u.shape[-1]
    u = lax.conv_general_dilated(
        u, conv_w[:, None, :].astype(u.dtype), window_strides=(1,),
        padding=[(CONV_W // 2, CONV_W // 2)],
        dimension_numbers=('NWC', 'WIO', 'NWC'), feature_group_count=C) + conv_b
    a, b = u[..., :D_FF], u[..., D_FF:]
    return (jax.nn.silu(a) * b) @ w_down


def setup_inputs(seed: int = 0) -> dict:
    key = jax.random.key(seed)
    ks = jax.random.split(key, 32)
    f32 = jnp.float32

    def w(k, shape, fan_in):
        return jax.random.normal(k, shape, f32) * (fan_in ** -0.5)

    def gain(k, shape):
        return 1.0 + 0.1 * jax.random.normal(k, shape, f32)

    Dl = DEPTH
    D = D_MODEL
    return {
        'x': jax.random.normal(ks[0], (BATCH, SEQ, D), f32),
        'mem': jax.random.normal(ks[1], (BATCH, MEM_LEN, D), f32),
        'norm_mix': gain(ks[2], (Dl, D)),
        'w_in': w(ks[3], (Dl, D, N_IN), D),
        'mla_q_norm': gain(ks[4], (Dl, MLA_Q_RANK)),
        'w_uq': w(ks[5], (Dl, MLA_Q_RANK, MLA_HEADS * (MLA_NOPE + MLA_ROPE)), MLA_Q_RANK),
        'mla_kv_norm': gain(ks[6], (Dl, MLA_KV_RANK)),
        'w_ukv': w(ks[7], (Dl, MLA_KV_RANK, MLA_HEADS * (MLA_NOPE + MLA_V)), MLA_KV_RANK),
        'gqa_q_norm': gain(ks[8], (Dl, GQA_HEAD_DIM)),
        'gqa_k_norm': gain(ks[9], (Dl, GQA_HEAD_DIM)),
        'w_o_mla': w(ks[10], (Dl, MIX_A_WIDTH, D), MIX_A_WIDTH),
        'w_o_gqa': w(ks[11], (Dl, MIX_B_WIDTH, D), MIX_B_WIDTH),
        'w_gate': w(ks[12], (Dl, D, 2 * D), D),
        'b_gate': 0.1 * jax.random.normal(ks[13], (Dl, 2 * D), f32),
        'w_out': w(ks[14], (Dl, D, D), D),
        'norm_cross': gain(ks[15], (Dl, D)),
        'norm_mem': gain(ks[16], (Dl, D)),
        'w_xq': w(ks[17], (Dl, D, X_HEADS * X_HEAD_DIM), D),
        'w_xkv': w(ks[18], (Dl, D, 2 * X_HEADS * X_HEAD_DIM), D),
        'w_xo': w(ks[19], (Dl, X_HEADS * X_HEAD_DIM, D), X_HEADS * X_HEAD_DIM),
        'norm_ffn': gain(ks[20], (Dl, D)),
        'w_up': w(ks[21], (Dl, D, 2 * D_FF), D),
        'conv_w': w(ks[22], (Dl, CONV_W, 2 * D_FF), CONV_W),
        'conv_b': 0.02 * jax.random.normal(ks[23], (Dl, 2 * D_FF), f32),
        'w_down': w(ks[24], (Dl, D_FF, D), D_FF),
        'norm_final': gain(ks[25], (D,)),
    }


def reference(x, mem, norm_mix, w_in, mla_q_norm, w_uq, mla_kv_norm, w_ukv,
              gqa_q_norm, gqa_k_norm, w_o_mla, w_o_gqa, w_gate, b_gate, w_out,
              norm_cross, norm_mem, w_xq, w_xkv, w_xo,
              norm_ffn, w_up, conv_w, conv_b, w_down, norm_final):
    L = x.shape[1]
    ROWS = L // GRID_W
    tpos = jnp.arange(L, dtype=jnp.float32)
    rows = jnp.repeat(jnp.arange(ROWS, dtype=jnp.float32), GRID_W)
    cols = jnp.tile(jnp.arange(GRID_W, dtype=jnp.float32), ROWS)

    for l in range(DEPTH):
        h = rms_norm(x, norm_mix[l])
        c_q, c_kv, k_r, qB, kB, vB = jnp.split(h @ w_in[l], IN_SPLITS, axis=-1)
        yA = mla_branch(c_q, c_kv, k_r, mla_q_norm[l], w_uq[l], mla_kv_norm[l], w_ukv[l], tpos)
        yB = gqa_branch(qB, kB, vB, gqa_q_norm[l], gqa_k_norm[l], rows, cols)
        g = jax.nn.sigmoid((h @ w_gate[l] + b_gate[l]).astype(jnp.float32)).astype(x.dtype)
        gA, gB = g[..., :D_MODEL], g[..., D_MODEL:]
        m = gA * (yA @ w_o_mla[l]) + gB * (yB @ w_o_gqa[l])
        x = x + m @ w_out[l]
        hc = rms_norm(x, norm_cross[l])
        mem_n = rms_norm(mem, norm_mem[l])
        x = x + memory_cross_attention(hc, mem_n, w_xq[l], w_xkv[l], w_xo[l])
        hf = rms_norm(x, norm_ffn[l])
        x = x + conv_glu_ffn(hf, w_up[l], conv_w[l], conv_b[l], w_down[l])

    return rms_norm(x, norm_final)
```

```python
from contextlib import ExitStack
from concourse.bass_utils import run_bass_kernel_spmd
import numpy as np
import concourse.bass as bass
import concourse.mybir as mybir

F32 = mybir.dt.float32
BF16 = mybir.dt.bfloat16
ALU = mybir.AluOpType
AF = mybir.ActivationFunctionType
AX = mybir.AxisListType

EPOCH = 30000
DMA_RING = {"sp": 8, "pool": 6, "act": 4}


class Op:
    __slots__ = ("eng", "idx", "emit", "deps", "is_dma", "signal", "dma_n", "tick", "barrier")

    def __init__(self, eng, idx, emit, is_dma):
        self.eng = eng
        self.idx = idx
        self.emit = emit
        self.deps = set()
        self.is_dma = is_dma
        self.signal = is_dma
        self.dma_n = -1
        self.tick = -1
        self.barrier = False


class _Recorder:
    def __init__(self):
        self.call = None

    def __getattr__(self, name):
        def f(*args, **kwargs):
            assert self.call is None, "emit closure must make exactly one engine call"
            self.call = (name, args, kwargs)
            return self
        return f


class Sched:
    STREAMS = ("pe", "act", "dve", "pool", "sp")

    def __init__(self, nc):
        self.nc = nc
        self.ops = {s: [] for s in self.STREAMS}
        self.res = {}
        self.dma_count = {s: 0 for s in self.STREAMS}
        self._bar_pos = {}
        self.stopped = False

    def add(self, eng, emit, reads=(), writes=(), dma=False):
        if self.stopped:
            return None
        lst = self.ops[eng]
        if emit is not None:
            rec = _Recorder()
            emit(rec)
            name_, args_, kwargs_ = rec.call
            emit = (lambda e, name_=name_, args_=args_, kwargs_=kwargs_: getattr(e, name_)(*args_, **kwargs_))
        op = Op(eng, len(lst), emit, dma)
        if dma:
            op.dma_n = self.dma_count[eng]
            self.dma_count[eng] += 1
        for k in reads:
            st = self.res.get(k)
            if st is None:
                st = self.res[k] = [None, []]
            if st[0] is not None:
                op.deps.add(st[0])
        for k in writes:
            st = self.res.get(k)
            if st is None:
                st = self.res[k] = [None, []]
            if st[0] is not None:
                op.deps.add(st[0])
            for r in st[1]:
                op.deps.add(r)
        for k in reads:
            self.res[k][1].append(op)
        for k in writes:
            st = self.res[k]
            st[0] = op
            st[1] = []
        op.deps.discard(op)
        lst.append(op)
        return op

    def barrier(self):
        if self.stopped:
            return
        lasts = []
        for s in self.STREAMS:
            for o in reversed(self.ops[s]):
                if o.emit is not None and not o.is_dma:
                    lasts.append(o)
                    break
        pend = []
        for s in self.STREAMS:
            for o in self.ops[s][self._bar_pos.get(s, 0):]:
                if o.is_dma:
                    pend.append(o)
        for s in self.STREAMS:
            op = self.add(s, None)
            op.barrier = True
            op.deps.update(lasts)
            op.deps.update(pend)
        for s in self.STREAMS:
            self._bar_pos[s] = len(self.ops[s])
        self.res = {}

    def finalize(self, ctx):
        nc = self.nc
        for s in self.STREAMS:
            for op in self.ops[s]:
                for d in op.deps:
                    if not d.is_dma:
                        if d.eng == op.eng:
                            if d.eng == "pe" and not op.barrier:
                                continue
                            if (op.idx - d.idx) > 2 and not op.barrier:
                                continue
                        d.signal = True
        self.prog_sems = {}
        for s in self.STREAMS:
            t = 0
            for op in self.ops[s]:
                if op.signal and not op.is_dma and op.emit is not None:
                    op.tick = t
                    t += 1
            n_ep = (t + EPOCH - 1) // EPOCH
            self.prog_sems[s] = [ctx.enter_context(nc.semaphore(f"pg_{s}_{e}")) for e in range(max(n_ep, 1))]
        self.dma_sems = {}
        for s in self.STREAMS:
            if self.dma_count[s]:
                self.dma_sems[s] = [ctx.enter_context(nc.semaphore(f"dq_{s}_{i}")) for i in range(DMA_RING[s])]

        def target(d):
            if d.is_dma:
                R = DMA_RING[d.eng]
                return self.dma_sems[d.eng][d.dma_n % R], 16 * (d.dma_n // R + 1)
            return self.prog_sems[d.eng][d.tick // EPOCH], d.tick % EPOCH + 1

        sched = self
        ctx.enter_context(nc.allow_non_contiguous_dma(reason="single-column halo transfers"))
        block = ctx.enter_context(nc.Block())

        def run_stream(s, eng):
            waited = {}

            def wait(sem, val):
                k = id(sem)
                if waited.get(k, 0) >= val:
                    return
                waited[k] = val
                eng.wait_ge(sem, val)

            for op in sched.ops[s]:
                for d in sorted(op.deps, key=lambda o: (o.eng, o.idx)):
                    if not d.is_dma:
                        if d.emit is None:
                            continue
                        if d.eng == s:
                            if s == "pe" and not op.barrier:
                                continue
                            if (op.idx - d.idx) > 2 and not op.barrier:
                                continue
                        if d.tick < 0:
                            continue
                    sem, val = target(d)
                    wait(sem, val)
                if op.emit is None:
                    continue
                if op.is_dma:
                    R = DMA_RING[s]
                    if op.dma_n >= R:
                        wait(sched.dma_sems[s][op.dma_n % R], 16 * (op.dma_n // R))
                    ins = op.emit(eng)
                    ins.then_inc(sched.dma_sems[s][op.dma_n % R], 16)
                else:
                    ins = op.emit(eng)
                    if op.signal:
                        ins.then_inc(sched.prog_sems[s][op.tick // EPOCH], 1)

        @block.tensor
        def _(e):
            run_stream("pe", e)

        @block.scalar
        def _(e):
            run_stream("act", e)

        @block.vector
        def _(e):
            run_stream("dve", e)

        @block.gpsimd
        def _(e):
            run_stream("pool", e)

        @block.sync
        def _(e):
            run_stream("sp", e)


P = 128
D = 2048
DC = 16
SEQ = 4096
NQ = 2048
NQH = NQ + 1
N_IN = 2624
DFF = 5632
EPS = 1e-6
SBUF_ELEMS = 104000

HALO_W = 2
QBLOCKS = [(0, 512), (512, 512), (1024, 512), (1536, 512), (2048, HALO_W)]


def _tiles_of(w):
    if w == 512:
        return [(i * 128, 128) for i in range(4)]
    return [(0, w)]


class Arena:
    def __init__(self, big, cap):
        self.big = big
        self.cap = cap
        self.off = 0
        self.peak = 0

    def mark(self):
        return self.off

    def release(self, m):
        self.off = m

    def __call__(self, shape, dtype, parts=None):
        parts = shape[0] if parts is None else parts
        n = 1
        for s_ in shape[1:]:
            n *= s_
        nb = n * 2 if dtype == F32 else n
        off = (self.off + 15) // 16 * 16
        assert off + nb <= self.cap, f"SBUF arena overflow: {off + nb} > {self.cap}"
        self.off = off + nb
        self.peak = max(self.peak, self.off)
        v = self.big[0:parts, off:off + nb]
        if dtype == F32:
            v = v.bitcast(F32)
        if len(shape) == 3:
            v = v.rearrange("p (a b) -> p a b", b=shape[2])
        elif len(shape) == 4:
            v = v.rearrange("p (a b c) -> p a b c", b=shape[2], c=shape[3])
        return v


class Rot:
    def __init__(self, items):
        self.items = list(items)
        self.i = 0

    def next(self):
        v = self.items[self.i % len(self.items)]
        self.i += 1
        return v


def build_program(dbg=False, stop_after=None, skip_to=None, p4_limit=None):
    nc = bass.Bass("TRN2", target_bir_lowering=False)

    def din(name, shape, dt=F32):
        return nc.dram_tensor(name, list(shape), dt, kind="ExternalInput").ap()

    xkv = din("xkv", [SEQ, D])
    mem = din("mem", [256, D])
    w_in = din("w_in", [D, N_IN])
    w_uq = din("w_uq", [512, 1536])
    w_ukv = din("w_ukv", [512, 2048])
    wqB_d = din("wqB_d", [8, P, DC, 128])
    gsl_d = din("gsl_d", [DC, P, 48, 128])
    wout_d = din("wout_d", [8, P, DC, 256])
    wxq_d = din("wxq_d", [2, P, DC, 256])
    wxkv_d = din("wxkv_d", [4, P, DC, 256])
    wxo_d = din("wxo_d", [4, P, 4, 512])
    wup_d = din("wup_d", [44, P, 32, 128])
    wdn_d = din("wdn_d", [4, 4, P, 11, 512])
    smallp = din("smallp", [P, 512])
    gfin = din("gfin", [1, D])
    ident_d = din("ident", [P, P])
    perm_d = din("perm", [P, P])
    rope1 = din("rope1", [2, 64, SEQ])
    rope2 = din("rope2", [2, P, SEQ])
    out = nc.dram_tensor("out", [NQ, D], F32, kind="ExternalOutput").ap()
    gsl_b = nc.dram_tensor("gsl_b", [DC, P, 48 * 128], BF16).ap()
    wout_b = nc.dram_tensor("wout_b", [8, P, DC * 256], BF16).ap()
    wxq_b = nc.dram_tensor("wxq_b", [2, P, DC * 256], BF16).ap()
    wxo_b = nc.dram_tensor("wxo_b", [4, P, 4 * 512], BF16).ap()
    wup_b = nc.dram_tensor("wup_b", [44, P, 32 * 128], BF16).ap()
    wdn_b = nc.dram_tensor("wdn_b", [16, P, 11 * 512], BF16).ap()
    skind = "ExternalOutput" if dbg else "Internal"
    yA_d = nc.dram_tensor("yA_d", [8, P, NQH + 7], BF16, kind=skind).ap()
    yB_d = nc.dram_tensor("yB_d", [8, P, NQH + 7], BF16, kind=skind).ap()
    x2_d = nc.dram_tensor("x2_d", [NQ + HALO_W, D], F32, kind=skind).ap()
    dbg_t = {}
    if dbg:
        for nm, shp, dt_ in (("d_ckvT", [P, 4 * SEQ], BF16), ("d_krT", [64, SEQ], BF16), ("d_kBT", [P, 2 * SEQ], BF16),
                             ("d_vB", [P, 32 * 256], BF16), ("d_qBT", [P, 8 * (NQH + 7)], BF16), ("d_cqT", [P, 4 * (NQH + 7)], BF16),
                             ("d_hT", [P, 16 * 512], BF16), ("d_hn", [P, 4 * D], BF16), ("d_rt", [P, 16], F32), ("d_ss", [P, 16], F32),
                             ("d_tmp", [P, 2048], F32), ("d_hfT", [P, 16 * 512], BF16), ("d_gT", [P, 44 * 512], BF16), ("d_acv", [P, 512], F32),
                             ("d_bcv", [P, 512], F32), ("d_x3", [P, 4 * D], F32), ("d_sq", [P, 2048], BF16), ("d_rstd", [P, 512], F32), ("d_xt", [P, D], F32)):
            dbg_t[nm] = nc.dram_tensor(nm, shp, dt_, kind="ExternalOutput").ap()

    def wview(w):
        return w.rearrange("(c p) n -> p c n", p=P)

    w_in_v, w_uq_v, w_ukv_v = wview(w_in), wview(w_uq), wview(w_ukv)

    ctx = ExitStack()
    with ctx:
        S = Sched(nc)
        big = nc.alloc_sbuf_tensor("arena", [P, SBUF_ELEMS], BF16)
        A = Arena(big, SBUF_ELEMS)
        psall = ctx.enter_context(nc.psum_tensor("psall", [P, 4096], F32))

        def bank(i, parts=P, w=512, off=0):
            return psall[0:parts, i * 512 + off: i * 512 + off + w]

        def bankbf(i, parts=P):
            return psall[0:parts, i * 512:(i + 1) * 512].bitcast(BF16)

        def bk(i):
            return ("pb", i)

        ident = A([P, P], BF16)
        perm = A([P, P], BF16)
        ones = A([P, P], BF16)
        sp_ = A([P, 512], F32)
        epst = A([P, 1], F32)
        S.add("pool", lambda e: e.dma_start(out=ident, in_=ident_d), writes=["ident"], dma=True)
        S.add("pool", lambda e: e.dma_start(out=perm, in_=perm_d), writes=["perm"], dma=True)
        S.add("sp", lambda e: e.dma_start(out=sp_, in_=smallp), writes=["smallp"], dma=True)
        S.add("dve", lambda e: e.memset(ones, 1.0), writes=["ones"])
        S.add("dve", lambda e: e.memset(epst, EPS), writes=["eps"])
        G_MIX, G_CROSS, G_MEM, G_FFN = 0, 16, 32, 48
        G_MLAQ, G_MLAKV, G_GQ, G_GK = 64, 68, 72, 73
        MASKL, MASKR = 74, 75
        BGATE = 80
        CONVB = 112
        CONVW = 200

        def spc(c, n=1):
            return sp_[:, c:c + n]

        CONST_R = ["ident", "perm", "ones", "smallp", "eps"]
        S.barrier()

        def stop_here(name, dumps=()):
            if stop_after != name:
                return
            S.barrier()
            for (nm, ap2d) in dumps:
                S.add("sp", lambda e, nm=nm, ap2d=ap2d: e.dma_start(out=dbg_t[nm], in_=ap2d), dma=True)
            S.barrier()
            S.stopped = True

        tog = {"i": 0}

        def evac_engine():
            tog["i"] += 1
            return "dve" if tog["i"] % 2 else "act"

        def copy_op(eng, out_ap, in_ap, reads, writes):
            if eng == "act":
                S.add("act", lambda e: e.copy(out=out_ap, in_=in_ap), reads=reads, writes=writes)
            else:
                S.add(eng, lambda e: e.tensor_copy(out=out_ap, in_=in_ap), reads=reads, writes=writes)

        def tm_norm_T(xtiles, hn, ss, rt, gain_col, dstT, dst_key, banks, ncols_total, layout, scale_eng="act", evacs=None):
            nt = len(xtiles)
            S.add("dve", lambda e: e.memset(ss[:, 0:8], 0.0), writes=["ss"])
            for g0 in range(0, nt, 2):
                grp = list(range(g0, min(g0 + 2, nt)))
                aps = {}
                for i in grp:
                    aps[i] = xtiles[i][0]()
                pmax = max(xtiles[i][2] for i in grp)
                for i in grp:
                    xa, xk, npp = aps[i], xtiles[i][1], xtiles[i][2]
                    S.add("act", lambda e, xa=xa, i=i, npp=npp: e.activation(
                        out=hn[0:npp, i, :], in_=xa, func=AF.Square, accum_out=ss[0:npp, i:i + 1]),
                        reads=[xk, "ss"], writes=[("hn", i), ("ssv", i)])
                c0, c1 = grp[0], grp[-1] + 1
                S.add("act", lambda e, c0=c0, c1=c1, pmax=pmax: e.activation(
                    out=rt[0:pmax, c0:c1], in_=ss[0:pmax, c0:c1], func=AF.Sqrt, scale=1.0 / D, bias=epst[0:pmax, :]),
                    reads=[("ssv", i) for i in grp] + ["eps", "ss"], writes=[("rt", g0)])
                S.add("dve", lambda e, c0=c0, c1=c1, pmax=pmax: e.reciprocal(out=rt[0:pmax, 8 + c0:8 + c1], in_=rt[0:pmax, c0:c1]),
                      reads=[("rt", g0)], writes=[("rs", g0)])
                for i in grp:
                    xa, xk, npp = aps[i], xtiles[i][1], xtiles[i][2]
                    if scale_eng == "act":
                        S.add("act", lambda e, xa=xa, i=i, npp=npp: e.activation(
                            out=hn[0:npp, i, :], in_=xa, func=AF.Identity, scale=rt[0:npp, 8 + i:9 + i]),
                            reads=[xk, ("rs", g0)], writes=[("hn", i)])
                    else:
                        S.add("dve", lambda e, xa=xa, i=i, npp=npp: e.tensor_scalar_mul(
                            out=hn[0:npp, i, :], in0=xa, scalar1=rt[0:npp, 8 + i:9 + i]),
                            reads=[xk, ("rs", g0)], writes=[("hn", i)])
            for c in range(DC):
                b = banks.next()
                pbf = bankbf(b)
                for (ti, p0, npp, dcol) in layout:
                    S.add("pe", lambda e, ti=ti, p0=p0, npp=npp, dcol=dcol, c=c, pbf=pbf: e.transpose(
                        out=pbf[:, dcol:dcol + npp], in_=hn[p0:p0 + npp, ti, c * 128:(c + 1) * 128],
                        identity=ident[p0:p0 + npp, p0:p0 + npp]),
                        reads=[("hn", ti), "ident"], writes=[bk(b)])
                S.add("dve", lambda e, c=c, pbf=pbf: e.tensor_scalar_mul(
                    out=dstT[:, c, 0:ncols_total], in0=pbf[:, 0:ncols_total], scalar1=spc(gain_col + c)), reads=[bk(b), "smallp"], writes=[(dst_key, c)])
                if evacs is not None:
                    evacs(c, pbf, b)

        def fm_rmsnorm(src_banks, nfeat, W, gain_col, tmp, sq, rstd, dsts, dst_keys, banks, parts=P, coff=0, rkey="fm_rstd"):
            n = len(src_banks)
            for c, b in enumerate(src_banks):
                S.add("act", lambda e, c=c, b=b: e.copy(out=tmp[:, coff + c, 0:W], in_=bank(b, w=W)),
                      reads=[bk(b)], writes=[("fm_tmp", coff + c)])
                S.add("dve", lambda e, c=c: e.tensor_tensor(out=sq[:, coff + c, 0:W], in0=tmp[:, coff + c, 0:W], in1=tmp[:, coff + c, 0:W],
                                                            op=ALU.mult), reads=[("fm_tmp", coff + c)], writes=[("fm_sq", coff + c)])
            bs = banks.next()
            for c in range(n):
                S.add("pe", lambda e, c=c, bs=bs: e.matmul(bank(bs, w=W), lhsT=ones, rhs=sq[:, coff + c, 0:W],
                                                          start=(c == 0), stop=(c == n - 1)),
                      reads=[("fm_sq", coff + c), "ones"], writes=[bk(bs)])
            S.add("act", lambda e, bs=bs: e.activation(out=rstd[:, 0:W], in_=bank(bs, w=W), func=AF.Sqrt,
                                                      scale=1.0 / nfeat, bias=epst), reads=[bk(bs), "eps"], writes=[rkey])
            S.add("dve", lambda e: e.reciprocal(out=rstd[:, 0:W], in_=rstd[:, 0:W]), reads=[rkey], writes=[rkey])
            for c in range(n):
                S.add("dve", lambda e, c=c: e.scalar_tensor_tensor(
                    out=dsts[c], in0=tmp[:, coff + c, 0:W], scalar=spc(gain_col + c), in1=rstd[:, 0:W],
                    op0=ALU.mult, op1=ALU.mult), reads=[("fm_tmp", coff + c), rkey, "smallp"], writes=[dst_keys[c]])

        def rope_fm(src_bf, src_key, np_, W, Ct, St, tab_keys, t1, t2, dst, dst_key, banks, tkey=0):
            b = banks.next()
            S.add("pe", lambda e, b=b: e.matmul(bank(b, parts=np_, w=W), lhsT=perm[0:np_, 0:np_], rhs=src_bf,
                                                start=True, stop=True), reads=[src_key, "perm"], writes=[bk(b)])
            S.add("dve", lambda e: e.tensor_tensor(out=t1[0:np_, 0:W], in0=src_bf, in1=Ct, op=ALU.mult),
                  reads=[src_key] + list(tab_keys), writes=[("rp_t1", tkey)])
            S.add("dve", lambda e, b=b: e.tensor_tensor(out=t2[0:np_, 0:W], in0=bank(b, parts=np_, w=W), in1=St, op=ALU.mult),
                  reads=[bk(b)] + list(tab_keys), writes=[("rp_t2", tkey)])
            S.add("dve", lambda e: e.tensor_tensor(out=dst, in0=t1[0:np_, 0:W], in1=t2[0:np_, 0:W], op=ALU.add),
                  reads=[("rp_t1", tkey), ("rp_t2", tkey)], writes=[dst_key])

        def load_x_tiles(xt_rot, row0, W):
            res = []
            for (t0, npp) in _tiles_of(W):
                xa, xk = xt_rot.next()

                def ld(xa=xa, xk=xk, npp=npp, r=row0 + t0):
                    S.add("sp", lambda e: e.dma_start(out=xa[0:npp, :], in_=xkv[r:r + npp, :]), writes=[xk], dma=True)
                    return xa[0:npp, :]
                res.append((ld, xk, npp))
            return res

        def std_layout(W):
            return [(i, 0, npp, t0) for i, (t0, npp) in enumerate(_tiles_of(W))]

        class AttnPipe:
            def __init__(self, la=2):
                self.tasks = []
                self.la = la

            def add(self, qk, ex, pv, end=None, pre=None):
                self.tasks.append((qk, ex, pv, end, pre))

            def run(self):
                n = len(self.tasks)
                for i in range(n + self.la):
                    if i < n:
                        if self.tasks[i][4] is not None:
                            self.tasks[i][4]()
                        self.tasks[i][0]()
                    j = i - self.la
                    if j >= 0:
                        self.tasks[j][1]()
                        self.tasks[j][2]()
                        if self.tasks[j][3] is not None:
                            self.tasks[j][3]()

        def attention_tasks(pipe, nkt, W, K1, K2, Q1, Q2, Vfn, scale, wbanks, pts, acc_o, acc_s, rc, finish,
                            rk, pre=None):
            for kt in range(nkt):
                b = wbanks.next()
                pt, pk = pts.next()

                def qk(kt=kt, b=b):
                    S.add("pe", lambda e: e.matmul(bank(b, w=W), lhsT=K1(kt), rhs=Q1, start=True, stop=(K2 is None)),
                          reads=rk["k"] + rk["q"], writes=[bk(b)])
                    if K2 is not None:
                        S.add("pe", lambda e: e.matmul(bank(b, w=W), lhsT=K2(kt), rhs=Q2, start=False, stop=True),
                              reads=rk["k"] + rk["q"], writes=[bk(b)])

                def ex(b=b, pt=pt, pk=pk):
                    S.add("act", lambda e: e.activation(out=pt[:, 0:W], in_=bank(b, w=W), func=AF.Exp, scale=scale),
                          reads=[bk(b)], writes=[pk])

                def pv(kt=kt, pt=pt, pk=pk):
                    S.add("pe", lambda e: e.matmul(bank(acc_o, w=W), lhsT=Vfn(kt), rhs=pt[:, 0:W],
                                                   start=(kt == 0), stop=(kt == nkt - 1)),
                          reads=rk["v"] + [pk], writes=[bk(acc_o)])
                    S.add("pe", lambda e: e.matmul(bank(acc_s, w=W), lhsT=ones, rhs=pt[:, 0:W],
                                                   start=(kt == 0), stop=(kt == nkt - 1)),
                          reads=["ones", pk], writes=[bk(acc_s)])

                end = None
                if kt == nkt - 1:
                    def end():
                        S.add("dve", lambda e: e.reciprocal(out=rc[:, 0:W], in_=bank(acc_s, w=W)),
                              reads=[bk(acc_s)], writes=["rc"])
                        finish()
                pipe.add(qk, ex, pv, end, pre if kt == 0 else None)

        m_att = A.mark()
        ckvT = A([P, 4, SEQ], BF16)
        krT = A([64, SEQ], BF16)
        m_gqa = A.mark()
        kBT = A([P, 2, SEQ], BF16)
        vB = A([P, 32, 256], BF16)
        m_p1 = A.mark()

        if skip_to is not None:
            S.stopped = True
        winkv = A([P, DC, 1088], BF16)
        xt = [A([P, D], F32) for _ in range(2)]
        xt_rot = Rot([(xt[i], ("xt", i)) for i in range(2)])
        hn = A([P, 4, D], BF16)
        hT = A([P, DC, 512], BF16)
        ss = A([P, 16], F32)
        rt = A([P, 16], F32)
        tabs = [[A([P, 512], F32) for _ in range(4)] for _ in range(2)]
        fm_tmp = A([P, 4, 512], F32)
        fm_sq = A([P, 4, 512], BF16)
        fm_rstd = A([P, 512], F32)
        nrm_bf = A([P, 512], BF16)
        rp_t1 = A([P, 512], F32)
        rp_t2 = A([P, 512], F32)
        banks = Rot(range(8))

        for q4 in range(4):
            S.add("pool", lambda e, q4=q4: e.dma_start(out=winkv[:, q4 * 4:(q4 + 1) * 4, 0:576],
                                                     in_=w_in_v[:, q4 * 4:(q4 + 1) * 4, 512:1088]),
                  writes=[("winkv", q4)], dma=True)
            S.add("pool", lambda e, q4=q4: e.dma_start(out=winkv[:, q4 * 4:(q4 + 1) * 4, 576:1088],
                                                     in_=w_in_v[:, q4 * 4:(q4 + 1) * 4, 2112:2624]),
                  writes=[("winkv", q4)], dma=True)
        WKV_R = [("winkv", q4) for q4 in range(4)]

        def load_tabs(tset, c0, W, need1=True, need2=True):
            tb = tabs[tset]
            if need1:
                S.add("sp", lambda e: e.dma_start(out=tb[0][0:64, 0:W], in_=rope1[0, :, c0:c0 + W]), writes=[("tab", tset, 0)], dma=True)
                S.add("sp", lambda e: e.dma_start(out=tb[1][0:64, 0:W], in_=rope1[1, :, c0:c0 + W]), writes=[("tab", tset, 1)], dma=True)
            if need2:
                S.add("sp", lambda e: e.dma_start(out=tb[2][:, 0:W], in_=rope2[0, :, c0:c0 + W]), writes=[("tab", tset, 2)], dma=True)
                S.add("sp", lambda e: e.dma_start(out=tb[3][:, 0:W], in_=rope2[1, :, c0:c0 + W]), writes=[("tab", tset, 3)], dma=True)

        for kb in range(8):
            c0 = kb * 512
            tset = kb % 2
            load_tabs(tset, c0, 512)
            xts = load_x_tiles(xt_rot, c0, 512)
            tm_norm_T(xts, hn, ss, rt, G_MIX, hT, "hT", banks, 512, std_layout(512))
            HT_R = [("hT", c) for c in range(DC)]
            cb = [banks.next() for _ in range(4)]
            for cc in range(4):
                for k in range(DC):
                    S.add("pe", lambda e, cc=cc, k=k, b=cb[cc]: e.matmul(
                        bank(b), lhsT=winkv[:, k, cc * 128:(cc + 1) * 128], rhs=hT[:, k, :],
                        start=(k == 0), stop=(k == DC - 1)), reads=[("hT", k), ("winkv", k // 4)], writes=[bk(cb[cc])])
            fm_rmsnorm(cb, 512, 512, G_MLAKV, fm_tmp, fm_sq, fm_rstd,
                       [ckvT[:, cc, c0:c0 + 512] for cc in range(4)], [("ckvT", cc, kb) for cc in range(4)], banks)
            b = banks.next()
            for k in range(DC):
                S.add("pe", lambda e, k=k, b=b: e.matmul(bank(b, parts=64), lhsT=winkv[:, k, 512:576], rhs=hT[:, k, :],
                                                         start=(k == 0), stop=(k == DC - 1)),
                      reads=[("hT", k), ("winkv", k // 4)], writes=[bk(b)])
            S.add("act", lambda e, b=b: e.copy(out=nrm_bf[0:64, :], in_=bank(b, parts=64)), reads=[bk(b)], writes=["nrm_bf"])
            rope_fm(nrm_bf[0:64, :], "nrm_bf", 64, 512, tabs[tset][0][0:64, :], tabs[tset][1][0:64, :], [("tab", tset, 0), ("tab", tset, 1)],
                    rp_t1, rp_t2, krT[:, c0:c0 + 512], ("krT", kb), banks)
            for g in range(2):
                b = banks.next()
                for k in range(DC):
                    S.add("pe", lambda e, k=k, b=b, g=g: e.matmul(
                        bank(b), lhsT=winkv[:, k, 576 + g * 128:576 + (g + 1) * 128], rhs=hT[:, k, :],
                        start=(k == 0), stop=(k == DC - 1)), reads=[("hT", k), ("winkv", k // 4)], writes=[bk(b)])
                fm_rmsnorm([b], 128, 512, G_GK, fm_tmp, fm_sq, fm_rstd, [nrm_bf[:, :]], ["nrm_bf"], banks)
                rope_fm(nrm_bf[:, :], "nrm_bf", P, 512, tabs[tset][2][:, :], tabs[tset][3][:, :], [("tab", tset, 2), ("tab", tset, 3)],
                        rp_t1, rp_t2, kBT[:, g, c0:c0 + 512], ("kBT", g, kb), banks)
            for pr in range(2):
                b = banks.next()
                for tt in range(2):
                    t = pr * 2 + tt
                    for k in range(DC):
                        S.add("pe", lambda e, k=k, b=b, t=t, tt=tt: e.matmul(
                            bank(b, w=256, off=tt * 256), lhsT=hT[:, k, t * 128:(t + 1) * 128], rhs=winkv[:, k, 832:1088],
                            start=(k == 0), stop=(k == DC - 1)), reads=[("hT", k), ("winkv", k // 4)], writes=[bk(b)])
                eng = evac_engine()
                copy_op(eng, vB[:, kb * 4 + pr * 2: kb * 4 + pr * 2 + 2, :].rearrange("p a b -> p (a b)"), bank(b),
                        [bk(b)], [("vB", kb)])
            if kb == 0:
                stop_here("p1b0", [("d_hT", hT.rearrange("p a b -> p (a b)")), ("d_hn", hn.rearrange("p a b -> p (a b)")), ("d_rt", rt), ("d_ss", ss),
                                   ("d_tmp", fm_tmp.rearrange("p a b -> p (a b)")), ("d_sq", fm_sq.rearrange("p a b -> p (a b)")),
                                   ("d_rstd", fm_rstd), ("d_xt", xt[1]), ("d_ckvT", ckvT.rearrange("p a b -> p (a b)")), ("d_krT", krT),
                                   ("d_vB", vB.rearrange("p a b -> p (a b)"))] if dbg else [])
        S.barrier()
        A.release(m_p1)

        stop_here("p1", [("d_ckvT", ckvT.rearrange("p a b -> p (a b)")), ("d_krT", krT), ("d_kBT", kBT.rearrange("p a b -> p (a b)")),
                         ("d_vB", vB.rearrange("p a b -> p (a b)"))] if dbg else [])
        qBT = A([P, 8, NQH + 7], BF16)
        m_p1 = A.mark()
        xt = [A([P, D], F32) for _ in range(2)]
        xt_rot = Rot([(xt[i], ("xt", i)) for i in range(2)])
        hn = A([P, 4, D], BF16)
        hT = A([P, DC, 512], BF16)
        ss = A([P, 16], F32)
        rt = A([P, 16], F32)
        tabs = [[None, None, A([P, 512], F32), A([P, 512], F32)] for _ in range(2)]
        fm_tmp = A([P, 4, 512], F32)
        fm_sq = A([P, 4, 512], BF16)
        fm_rstd = A([P, 512], F32)
        nrm_bf = A([P, 512], BF16)
        rp_t1 = A([P, 512], F32)
        rp_t2 = A([P, 512], F32)
        qslab = [A([P, DC, 128], BF16) for _ in range(2)]
        qs_rot = Rot([(qslab[i], ("qslab", i)) for i in range(2)])
        scr2 = (A([P, 512], F32), A([P, 512], BF16), A([P, 512], F32), A([P, 512], F32))
        banks = Rot(range(8))
        for qi, (q0, W) in enumerate(QBLOCKS):
            tset = qi % 2
            load_tabs(tset, q0, W, need1=False)
            xts = load_x_tiles(xt_rot, q0, W)
            tm_norm_T(xts, hn, ss, rt, G_MIX, hT, "hT", banks, W, std_layout(W))
            for h in range(8):
                sl, sk = qs_rot.next()
                S.add("pool", lambda e, sl=sl, h=h: e.dma_start(out=sl, in_=wqB_d[h]), writes=[sk], dma=True)
                b = banks.next()
                for k in range(DC):
                    S.add("pe", lambda e, k=k, b=b, sl=sl, W=W: e.matmul(bank(b, w=W), lhsT=sl[:, k, :], rhs=hT[:, k, 0:W],
                                                                       start=(k == 0), stop=(k == DC - 1)),
                          reads=[("hT", k), sk], writes=[bk(b)])
                hs = h % 2
                rs_, nb_, t1_, t2_ = (fm_rstd, nrm_bf, rp_t1, rp_t2) if hs == 0 else scr2
                fm_rmsnorm([b], 128, W, G_GQ, fm_tmp, fm_sq, rs_, [nb_[:, 0:W]], [("nrm_bf", hs)], banks, coff=h % 4, rkey=("fm_rstd", hs))
                rope_fm(nb_[:, 0:W], ("nrm_bf", hs), P, W, tabs[tset][2][:, 0:W], tabs[tset][3][:, 0:W], [("tab", tset, 2), ("tab", tset, 3)],
                        t1_, t2_, qBT[:, h, q0:q0 + W], ("qBT", h, qi), banks, tkey=hs)
        S.barrier()
        A.release(m_p1)

        stop_here("p2a", [("d_qBT", qBT.rearrange("p a b -> p (a b)")), ("d_ckvT", ckvT.rearrange("p a b -> p (a b)")), ("d_krT", krT),
                          ("d_kBT", kBT.rearrange("p a b -> p (a b)")), ("d_vB", vB.rearrange("p a b -> p (a b)"))] if dbg else [])
        cast_a = [(gsl_b[f], gsl_d[f].rearrange("p a b -> p (a b)")) for f in range(DC)]
        cast_a += [(wout_b[c], wout_d[c].rearrange("p a b -> p (a b)")) for c in range(8)]
        cast_a += [(wxq_b[c], wxq_d[c].rearrange("p a b -> p (a b)")) for c in range(2)]
        cast_a += [(wxo_b[c], wxo_d[c].rearrange("p a b -> p (a b)")) for c in range(4)]
        cast_b = [(wup_b[j], wup_d[j].rearrange("p a b -> p (a b)")) for j in range(44)]
        cast_b += [(wdn_b[c * 4 + s4], wdn_d[c, s4].rearrange("p a b -> p (a b)")) for c in range(4) for s4 in range(4)]

        def issue_casts(lst):
            for (dst, src) in lst:
                S.add("pool", lambda e, dst=dst, src=src: e.dma_start(out=dst, in_=src), dma=True)

        if skip_to is None:
            issue_casts(cast_a)

        pts_l = [A([P, 512], BF16) for _ in range(4)]
        rc = A([P, 512], F32)
        stg = [A([P, 512], BF16) for _ in range(2)]
        stg_rot = Rot([(stg[i], ("stg", i)) for i in range(2)])
        pts = Rot([(pts_l[i], ("pt", i)) for i in range(4)])
        wbanks = Rot(range(4))
        acc_rot = Rot([(4, 6), (5, 7)])
        pipe = AttnPipe()
        for h in range(8):
            g = h // 4
            for qi, (q0, W) in enumerate(QBLOCKS):
                ao, as_ = acc_rot.next()

                def finish(h=h, q0=q0, W=W, ao=ao):
                    sg, sgk = stg_rot.next()
                    S.add("dve", lambda e: e.tensor_tensor(out=sg[:, 0:W], in0=bank(ao, w=W), in1=rc[:, 0:W], op=ALU.mult),
                          reads=[bk(ao), "rc"], writes=[sgk])
                    S.add("sp", lambda e: e.dma_start(out=yB_d[h, :, q0:q0 + W], in_=sg[:, 0:W]), reads=[sgk],
                          writes=[("yB_d", h, q0)], dma=True)

                attention_tasks(
                    pipe, 32, W,
                    K1=lambda kt, g=g: kBT[:, g, kt * 128:(kt + 1) * 128], K2=None,
                    Q1=qBT[:, h, q0:q0 + W], Q2=None,
                    Vfn=lambda kt, g=g: vB[:, kt, g * 128:(g + 1) * 128], scale=float(128 ** -0.5),
                    wbanks=wbanks, pts=pts, acc_o=ao, acc_s=as_, rc=rc, finish=finish,
                    rk={"k": [], "q": [], "v": []})
        pipe.run()
        S.barrier()
        A.release(m_gqa)

        stop_here("p3")
        cqT = A([P, 4, NQH + 7], BF16)
        m_p2b = A.mark()
        winq = A([P, DC, 512], BF16)
        xt = [A([P, D], F32) for _ in range(2)]
        xt_rot = Rot([(xt[i], ("xt", i)) for i in range(2)])
        hn = A([P, 4, D], BF16)
        hT = A([P, DC, 512], BF16)
        ss = A([P, 16], F32)
        rt = A([P, 16], F32)
        fm_tmp = A([P, 4, 512], F32)
        fm_sq = A([P, 4, 512], BF16)
        fm_rstd = A([P, 512], F32)
        banks = Rot(range(8))
        for q4 in range(4):
            S.add("pool", lambda e, q4=q4: e.dma_start(out=winq[:, q4 * 4:(q4 + 1) * 4, :], in_=w_in_v[:, q4 * 4:(q4 + 1) * 4, 0:512]),
                  writes=[("winq", q4)], dma=True)
        for qi, (q0, W) in enumerate(QBLOCKS):
            xts = load_x_tiles(xt_rot, q0, W)
            tm_norm_T(xts, hn, ss, rt, G_MIX, hT, "hT", banks, W, std_layout(W))
            cb = [banks.next() for _ in range(4)]
            for cc in range(4):
                for k in range(DC):
                    S.add("pe", lambda e, cc=cc, k=k, b=cb[cc], W=W: e.matmul(
                        bank(b, w=W), lhsT=winq[:, k, cc * 128:(cc + 1) * 128], rhs=hT[:, k, 0:W],
                        start=(k == 0), stop=(k == DC - 1)), reads=[("hT", k), ("winq", k // 4)], writes=[bk(cb[cc])])
            fm_rmsnorm(cb, 512, W, G_MLAQ, fm_tmp, fm_sq, fm_rstd,
                       [cqT[:, cc, q0:q0 + W] for cc in range(4)], [("cqT", cc, qi) for cc in range(4)], banks)
        S.barrier()
        A.release(m_p2b)

        stop_here("p2b", [("d_cqT", cqT.rearrange("p a b -> p (a b)"))] if dbg else [])
        if skip_to == "p4":
            S.stopped = False
            S.add("dve", lambda e: e.memset(ckvT, 0.5), writes=["x1"])
            S.add("dve", lambda e: e.memset(krT, 0.25), writes=["x2"])
            S.add("dve", lambda e: e.memset(cqT, 0.125), writes=["x3"])
            S.barrier()
        Kn = [A([P, SEQ], BF16) for _ in range(2)]
        Vh = [A([P, 32, 128], BF16) for _ in range(2)]
        qn = [A([P, NQH + 7], BF16) for _ in range(2)]
        qr = [A([64, NQH + 7], BF16) for _ in range(2)]
        wq_h = [A([P, 4, 192], BF16) for _ in range(2)]
        wkv_h = [A([P, 4, 256], BF16) for _ in range(2)]
        qtab = [[A([64, 512], F32) for _ in range(2)] for _ in range(2)]
        qr_tmp = A([64, 512], BF16)
        rp_t1 = A([P, 512], F32)
        rp_t2 = A([P, 512], F32)
        pts_l = [A([P, 512], BF16) for _ in range(4)]
        rc = A([P, 512], F32)
        stg = [A([P, 512], BF16) for _ in range(2)]
        stg_rot = Rot([(stg[i], ("stg", i)) for i in range(2)])
        pts = Rot([(pts_l[i], ("pt", i)) for i in range(4)])
        wbanks = Rot(range(4))
        acc_rot = Rot([(4, 6), (5, 7)])
        pipe = AttnPipe()
        qt_i = {"i": 0}

        def mla_head_prep(h):
            hb = h % 2
            S.add("pool", lambda e: e.dma_start(out=wq_h[hb], in_=w_uq_v[:, :, h * 192:(h + 1) * 192]), writes=[("wq_h", hb)], dma=True)
            S.add("pool", lambda e: e.dma_start(out=wkv_h[hb], in_=w_ukv_v[:, :, h * 256:(h + 1) * 256]), writes=[("wkv_h", hb)], dma=True)
            if h + 1 < 8:
                pass
            issue_casts(cast_b[h * 8:(h + 1) * 8] if h < 7 else cast_b[56:])
            for kb in range(8):
                b = wbanks.next()
                for kc in range(4):
                    S.add("pe", lambda e, kc=kc, b=b, kb=kb: e.matmul(
                        bank(b), lhsT=wkv_h[hb][:, kc, 0:128], rhs=ckvT[:, kc, kb * 512:(kb + 1) * 512],
                        start=(kc == 0), stop=(kc == 3)), reads=[("wkv_h", hb)], writes=[bk(b)])
                copy_op(evac_engine(), Kn[hb][:, kb * 512:(kb + 1) * 512], bank(b), [bk(b)], [("Kn", hb)])
            for g4 in range(8):
                b = wbanks.next()
                for j in range(4):
                    kt = g4 * 4 + j
                    for kc in range(4):
                        S.add("pe", lambda e, kc=kc, b=b, kt=kt, j=j: e.matmul(
                            bank(b, w=128, off=j * 128), lhsT=ckvT[:, kc, kt * 128:(kt + 1) * 128], rhs=wkv_h[hb][:, kc, 128:256],
                            start=(kc == 0), stop=(kc == 3)), reads=[("wkv_h", hb)], writes=[bk(b)])
                copy_op(evac_engine(), Vh[hb][:, g4 * 4:(g4 + 1) * 4, :].rearrange("p a b -> p (a b)"), bank(b), [bk(b)], [("Vh", hb)])
            for qi, (q0, W) in enumerate(QBLOCKS):
                ts_ = qt_i["i"] % 2
                qt_i["i"] += 1
                S.add("sp", lambda e, ts_=ts_, q0=q0, W=W: e.dma_start(out=qtab[ts_][0][:, 0:W], in_=rope1[0, :, q0:q0 + W]),
                      writes=[("qtab", ts_, 0)], dma=True)
                S.add("sp", lambda e, ts_=ts_, q0=q0, W=W: e.dma_start(out=qtab[ts_][1][:, 0:W], in_=rope1[1, :, q0:q0 + W]),
                      writes=[("qtab", ts_, 1)], dma=True)
                b = wbanks.next()
                for kc in range(4):
                    S.add("pe", lambda e, kc=kc, b=b, q0=q0, W=W: e.matmul(
                        bank(b, w=W), lhsT=wq_h[hb][:, kc, 0:128], rhs=cqT[:, kc, q0:q0 + W],
                        start=(kc == 0), stop=(kc == 3)), reads=[("wq_h", hb)], writes=[bk(b)])
                copy_op(evac_engine(), qn[hb][:, q0:q0 + W], bank(b, w=W), [bk(b)], [("qn", hb)])
                b2 = wbanks.next()
                for kc in range(4):
                    S.add("pe", lambda e, kc=kc, b2=b2, q0=q0, W=W: e.matmul(
                        bank(b2, parts=64, w=W), lhsT=wq_h[hb][:, kc, 128:192], rhs=cqT[:, kc, q0:q0 + W],
                        start=(kc == 0), stop=(kc == 3)), reads=[("wq_h", hb)], writes=[bk(b2)])
                S.add("act", lambda e, b2=b2, W=W: e.copy(out=qr_tmp[:, 0:W], in_=bank(b2, parts=64, w=W)), reads=[bk(b2)], writes=["qr_tmp"])
                rope_fm(qr_tmp[:, 0:W], "qr_tmp", 64, W, qtab[ts_][0][:, 0:W], qtab[ts_][1][:, 0:W], [("qtab", ts_, 0), ("qtab", ts_, 1)],
                        rp_t1, rp_t2, qr[hb][:, q0:q0 + W], ("qr", hb), wbanks)

        if p4_limit == "prep":
            mla_head_prep(0)
        for h in range(8):
            hb = h % 2
            if p4_limit == "prep" or (p4_limit is not None and p4_limit.startswith("h0") and h > 0):
                break
            if p4_limit is not None and p4_limit.startswith("n") and h >= int(p4_limit[1:]):
                break
            for qi, (q0, W) in enumerate(QBLOCKS):
                if p4_limit == "h0q0" and qi > 0:
                    break
                if p4_limit == "h0q3" and qi > 3:
                    break
                ao, as_ = acc_rot.next()

                def finish(h=h, q0=q0, W=W, ao=ao):
                    sg, sgk = stg_rot.next()
                    S.add("dve", lambda e: e.tensor_tensor(out=sg[:, 0:W], in0=bank(ao, w=W), in1=rc[:, 0:W], op=ALU.mult),
                          reads=[bk(ao), "rc"], writes=[sgk])
                    S.add("sp", lambda e: e.dma_start(out=yA_d[h, :, q0:q0 + W], in_=sg[:, 0:W]), reads=[sgk],
                          writes=[("yA_d", h, q0)], dma=True)

                attention_tasks(
                    pipe, 32, W,
                    K1=lambda kt, hb=hb: Kn[hb][:, kt * 128:(kt + 1) * 128],
                    K2=lambda kt: krT[:, kt * 128:(kt + 1) * 128],
                    Q1=qn[hb][:, q0:q0 + W], Q2=qr[hb][:, q0:q0 + W],
                    Vfn=lambda kt, hb=hb: Vh[hb][:, kt, :], scale=float(192 ** -0.5),
                    wbanks=wbanks, pts=pts, acc_o=ao, acc_s=as_, rc=rc, finish=finish,
                    rk={"k": [("Kn", hb)], "q": [("qn", hb), ("qr", hb)], "v": [("Vh", hb)]},
                    pre=(lambda h=h: mla_head_prep(h)) if qi == 0 else None)
            pipe.run()
            pipe = AttnPipe()
        S.barrier()
        A.release(m_att)

        stop_here("p4")
        KxT = A([P, 4, 256], BF16)
        Vx = A([P, 2, 512], BF16)
        xres = A([P, 4, D], F32)
        hn = A([P, 4, D], BF16)
        hT = A([P, DC, 512], BF16)
        mT = A([P, DC, 512], BF16)
        yAs = A([P, 8, 512], BF16)
        yBs = A([P, 8, 512], BF16)
        ss = A([P, 16], F32)
        rt = A([P, 16], F32)
        gslab = [A([P, 48, 128], BF16) for _ in range(2)]
        gs_rot = Rot([(gslab[i], ("gslab", i)) for i in range(2)])
        wslab = [A([P, DC, 256], BF16) for _ in range(3)]
        ws_rot = Rot([(wslab[i], ("wslab", i)) for i in range(3)])
        oslab = [A([P, 4, 512], BF16) for _ in range(2)]
        os_rot = Rot([(oslab[i], ("oslab", i)) for i in range(2)])
        gA = A([P, 512], F32)
        gB = A([P, 512], F32)
        gt1 = A([P, 512], F32)
        gt2 = A([P, 512], F32)
        qxT = A([P, 4, 512], BF16)
        oxT = A([P, 4, 512], BF16)
        pts_l = [A([P, 512], BF16) for _ in range(4)]
        pts = Rot([(pts_l[i], ("pt", i)) for i in range(4)])
        rc = A([P, 512], F32)
        banks = Rot(range(8))

        mts = []
        for t in range(2):
            S.add("sp", lambda e, t=t: e.dma_start(out=xres[:, t, :], in_=mem[t * 128:(t + 1) * 128, :]), writes=[("xres", t)], dma=True)
            mts.append((lambda t=t: xres[:, t, :], ("xres", t), P))
        tm_norm_T(mts, hn, ss, rt, G_MEM, hT, "hT", banks, 256, [(0, 0, P, 0), (1, 0, P, 128)])
        for half in range(2):
            for q2 in range(2):
                sl, sk = ws_rot.next()
                S.add("pool", lambda e, sl=sl, half=half, q2=q2: e.dma_start(out=sl, in_=wxkv_d[half * 2 + q2]), writes=[sk], dma=True)
                hh = half * 2 + q2
                b = banks.next()
                for k in range(DC):
                    S.add("pe", lambda e, k=k, b=b, sl=sl: e.matmul(bank(b, w=256), lhsT=sl[:, k, 0:128], rhs=hT[:, k, 0:256],
                                                                 start=(k == 0), stop=(k == DC - 1)), reads=[("hT", k), sk], writes=[bk(b)])
                copy_op(evac_engine(), KxT[:, hh, :], bank(b, w=256), [bk(b)], [("KxT", hh)])
                b = banks.next()
                for t in range(2):
                    for k in range(DC):
                        S.add("pe", lambda e, k=k, b=b, sl=sl, t=t: e.matmul(
                            bank(b, w=128, off=t * 128), lhsT=hT[:, k, t * 128:(t + 1) * 128], rhs=sl[:, k, 128:256],
                            start=(k == 0), stop=(k == DC - 1)), reads=[("hT", k), sk], writes=[bk(b)])
                for t in range(2):
                    copy_op(evac_engine(), Vx[:, t, hh * 128:(hh + 1) * 128], bank(b, w=128, off=t * 128), [bk(b)], [("Vx", hh)])

        for qi, (q0, W) in enumerate(QBLOCKS):
            tl = _tiles_of(W)
            xts = []
            for i, (t0, npp) in enumerate(tl):
                S.add("sp", lambda e, i=i, npp=npp, r=q0 + t0: e.dma_start(out=xres[0:npp, i, :], in_=xkv[r:r + npp, :]),
                      writes=[("xres", i)], dma=True)
                xts.append((lambda i=i, npp=npp: xres[0:npp, i, :], ("xres", i), npp))
            S.add("sp", lambda e, q0=q0, W=W: e.dma_start(out=yAs[:, :, 0:W], in_=yA_d[:, :, q0:q0 + W].rearrange("h p w -> p h w")),
                  reads=[("yA_d", h, q0) for h in range(8)], writes=["yAs"], dma=True)
            S.add("sp", lambda e, q0=q0, W=W: e.dma_start(out=yBs[:, :, 0:W], in_=yB_d[:, :, q0:q0 + W].rearrange("h p w -> p h w")),
                  reads=[("yB_d", h, q0) for h in range(8)], writes=["yBs"], dma=True)
            tm_norm_T(xts, hn, ss, rt, G_MIX, hT, "hT", banks, W, std_layout(W))
            for f in range(DC):
                sl, sk = gs_rot.next()
                S.add("pool", lambda e, sl=sl, f=f: e.dma_start(out=sl, in_=gsl_b[f].rearrange("p (a b) -> p a b", b=128)), writes=[sk], dma=True)
                bGA, bGB, bYA, bYB = [banks.next() for _ in range(4)]
                for k in range(DC):
                    S.add("pe", lambda e, k=k, b=bGA, sl=sl, W=W: e.matmul(bank(b, w=W), lhsT=sl[:, k, :], rhs=hT[:, k, 0:W],
                                                                        start=(k == 0), stop=(k == DC - 1)), reads=[("hT", k), sk], writes=[bk(bGA)])
                for k in range(DC):
                    S.add("pe", lambda e, k=k, b=bGB, sl=sl, W=W: e.matmul(bank(b, w=W), lhsT=sl[:, 16 + k, :], rhs=hT[:, k, 0:W],
                                                                        start=(k == 0), stop=(k == DC - 1)), reads=[("hT", k), sk], writes=[bk(bGB)])
                for hh in range(8):
                    S.add("pe", lambda e, hh=hh, b=bYA, sl=sl, W=W: e.matmul(bank(b, w=W), lhsT=sl[:, 32 + hh, :], rhs=yAs[:, hh, 0:W],
                                                                          start=(hh == 0), stop=(hh == 7)), reads=["yAs", sk], writes=[bk(bYA)])
                for hh in range(8):
                    S.add("pe", lambda e, hh=hh, b=bYB, sl=sl, W=W: e.matmul(bank(b, w=W), lhsT=sl[:, 40 + hh, :], rhs=yBs[:, hh, 0:W],
                                                                          start=(hh == 0), stop=(hh == 7)), reads=["yBs", sk], writes=[bk(bYB)])
                S.add("act", lambda e, b=bGA, f=f, W=W: e.activation(out=gA[:, 0:W], in_=bank(b, w=W), func=AF.Sigmoid, bias=spc(BGATE + f)),
                      reads=[bk(bGA), "smallp"], writes=["gA"])
                S.add("act", lambda e, b=bGB, f=f, W=W: e.activation(out=gB[:, 0:W], in_=bank(b, w=W), func=AF.Sigmoid, bias=spc(BGATE + 16 + f)),
                      reads=[bk(bGB), "smallp"], writes=["gB"])
                S.add("dve", lambda e, b=bYA, W=W: e.tensor_tensor(out=gt1[:, 0:W], in0=bank(b, w=W), in1=gA[:, 0:W], op=ALU.mult),
                      reads=[bk(bYA), "gA"], writes=["gt1"])
                S.add("dve", lambda e, b=bYB, W=W: e.tensor_tensor(out=gt2[:, 0:W], in0=bank(b, w=W), in1=gB[:, 0:W], op=ALU.mult),
                      reads=[bk(bYB), "gB"], writes=["gt2"])
                S.add("dve", lambda e, f=f, W=W: e.tensor_tensor(out=mT[:, f, 0:W], in0=gt1[:, 0:W], in1=gt2[:, 0:W], op=ALU.add),
                      reads=["gt1", "gt2"], writes=[("mT", f)])
            for cg in range(8):
                sl, sk = ws_rot.next()
                S.add("pool", lambda e, sl=sl, cg=cg: e.dma_start(out=sl, in_=wout_b[cg].rearrange("p (a b) -> p a b", b=256)), writes=[sk], dma=True)
                for i, (t0, npp) in enumerate(tl):
                    b = banks.next()
                    for k in range(DC):
                        S.add("pe", lambda e, k=k, b=b, sl=sl, t0=t0, npp=npp: e.matmul(
                            bank(b, parts=npp, w=256), lhsT=mT[:, k, t0:t0 + npp], rhs=sl[:, k, :],
                            start=(k == 0), stop=(k == DC - 1)), reads=[("mT", k), sk], writes=[bk(b)])
                    S.add("dve", lambda e, b=b, i=i, npp=npp, cg=cg: e.tensor_tensor(
                        out=xres[0:npp, i, cg * 256:(cg + 1) * 256], in0=bank(b, parts=npp, w=256),
                        in1=xres[0:npp, i, cg * 256:(cg + 1) * 256], op=ALU.add), reads=[bk(b), ("xres", i)], writes=[("xres", i)])
            tm_norm_T(xts, hn, ss, rt, G_CROSS, hT, "hT", banks, W, std_layout(W))
            for c2 in range(2):
                sl, sk = ws_rot.next()
                S.add("pool", lambda e, sl=sl, c2=c2: e.dma_start(out=sl, in_=wxq_b[c2].rearrange("p (a b) -> p a b", b=256)), writes=[sk], dma=True)
                for c1 in range(2):
                    cc = c2 * 2 + c1
                    b = banks.next()
                    for k in range(DC):
                        S.add("pe", lambda e, k=k, b=b, sl=sl, c1=c1, W=W: e.matmul(
                            bank(b, w=W), lhsT=sl[:, k, c1 * 128:(c1 + 1) * 128], rhs=hT[:, k, 0:W],
                            start=(k == 0), stop=(k == DC - 1)), reads=[("hT", k), sk], writes=[bk(b)])
                    copy_op(evac_engine(), qxT[:, cc, 0:W], bank(b, w=W), [bk(b)], [("qxT", cc)])
            pipe = AttnPipe()
            xw = Rot([0, 1, 2, 3])
            xacc = Rot([(4, 6), (5, 7)])
            for hh in range(4):
                ao, as_ = xacc.next()

                def finish(hh=hh, W=W, ao=ao):
                    S.add("dve", lambda e: e.tensor_tensor(out=oxT[:, hh, 0:W], in0=bank(ao, w=W), in1=rc[:, 0:W], op=ALU.mult),
                          reads=[bk(ao), "rc"], writes=[("oxT", hh)])

                attention_tasks(
                    pipe, 2, W,
                    K1=lambda kt, hh=hh: KxT[:, hh, kt * 128:(kt + 1) * 128], K2=None,
                    Q1=qxT[:, hh, 0:W], Q2=None,
                    Vfn=lambda kt, hh=hh: Vx[:, kt, hh * 128:(hh + 1) * 128], scale=float(128 ** -0.5),
                    wbanks=xw, pts=pts, acc_o=ao, acc_s=as_, rc=rc, finish=finish,
                    rk={"k": [("KxT", hh)], "q": [("qxT", hh)], "v": [("Vx", hh)]})
            pipe.run()
            banks = Rot(range(8))
            for cg in range(4):
                sl, sk = os_rot.next()
                S.add("pool", lambda e, sl=sl, cg=cg: e.dma_start(out=sl, in_=wxo_b[cg].rearrange("p (a b) -> p a b", b=512)), writes=[sk], dma=True)
                for i, (t0, npp) in enumerate(tl):
                    b = banks.next()
                    for k in range(4):
                        S.add("pe", lambda e, k=k, b=b, sl=sl, t0=t0, npp=npp: e.matmul(
                            bank(b, parts=npp), lhsT=oxT[:, k, t0:t0 + npp], rhs=sl[:, k, :],
                            start=(k == 0), stop=(k == 3)), reads=[("oxT", k), sk], writes=[bk(b)])
                    S.add("dve", lambda e, b=b, i=i, npp=npp, cg=cg: e.tensor_tensor(
                        out=xres[0:npp, i, cg * 512:(cg + 1) * 512], in0=bank(b, parts=npp),
                        in1=xres[0:npp, i, cg * 512:(cg + 1) * 512], op=ALU.add), reads=[bk(b), ("xres", i)], writes=[("xres", i)])
            for i, (t0, npp) in enumerate(tl):
                S.add("sp", lambda e, i=i, npp=npp, r=q0 + t0: e.dma_start(out=x2_d[r:r + npp, :], in_=xres[0:npp, i, :]),
                      reads=[("xres", i)], writes=["x2_d"], dma=True)
        S.barrier()
        A.release(m_att)

        stop_here("p5a")
        xres = A([P, 4, D], F32)
        nbx = A([33, D], F32)
        hn = A([P, 5, D], BF16)
        hfT = A([P, DC, 512], BF16)
        hfN = A([P, DC, 2], BF16)
        gT = A([P, 44, 512], BF16)
        ss = A([P, 16], F32)
        rt = A([P, 16], F32)
        uslab = [A([P, 32, 128], BF16) for _ in range(3)]
        us_rot = Rot([(uslab[i], ("uslab", i)) for i in range(3)])
        dslab = [A([P, 11, 512], BF16) for _ in range(3)]
        ds_rot = Rot([(dslab[i], ("dslab", i)) for i in range(3)])
        acv = [A([P, 512], F32) for _ in range(2)]
        bcv = [A([P, 512], F32) for _ in range(2)]
        sav = [A([P, 512], F32) for _ in range(2)]
        gfb = A([P, D], F32)
        gm = [A([P, DC, 2], F32) for _ in range(3)]
        S.add("sp", lambda e: e.dma_start(out=gfb, in_=gfin.partition_broadcast(P)), writes=["gfb"], dma=True)
        S.add("dve", lambda e: e.memset(nbx, 0.0), writes=["nbx"])
        for side in range(2):
            S.add("dve", lambda e, side=side: e.tensor_copy(out=gm[1][:, :, side], in_=sp_[:, G_FFN:G_FFN + DC]), reads=["smallp"], writes=["gm1"])
        S.add("dve", lambda e: e.tensor_scalar_mul(out=gm[0][:, :, 0], in0=sp_[:, G_FFN:G_FFN + DC], scalar1=spc(MASKL)), reads=["smallp"], writes=["gm0"])
        S.add("dve", lambda e: e.tensor_copy(out=gm[0][:, :, 1], in_=sp_[:, G_FFN:G_FFN + DC]), reads=["smallp"], writes=["gm0"])
        S.add("dve", lambda e: e.tensor_copy(out=gm[2][:, :, 0], in_=sp_[:, G_FFN:G_FFN + DC]), reads=["smallp"], writes=["gm2"])
        S.add("dve", lambda e: e.tensor_scalar_mul(out=gm[2][:, :, 1], in0=sp_[:, G_FFN:G_FFN + DC], scalar1=spc(MASKR)), reads=["smallp"], writes=["gm2"])
        banks = Rot(range(8))
        for j in range(4):
            r0 = j * 512
            li = r0 - 1 if j > 0 else NQ
            ri = r0 + 512 if j < 3 else NQ
            gmi = 0 if j == 0 else (2 if j == 3 else 1)
            xts = []
            for i in range(4):
                S.add("sp", lambda e, i=i, r=r0 + i * 128: e.dma_start(out=xres[:, i, :], in_=x2_d[r:r + 128, :]),
                      reads=["x2_d"], writes=[("xres", i)], dma=True)
                xts.append((lambda i=i: xres[:, i, :], ("xres", i), P))
            S.add("sp", lambda e, li=li: e.dma_start(out=nbx[0:1, :], in_=x2_d[li:li + 1, :]), reads=["x2_d"], writes=["nbx"], dma=True)
            S.add("sp", lambda e, ri=ri: e.dma_start(out=nbx[32:33, :], in_=x2_d[ri:ri + 1, :]), reads=["x2_d"], writes=["nbx"], dma=True)
            xts.append((lambda: nbx[0:33, :], "nbx", 33))
            layout = [(i, 0, P, i * 128) for i in range(4)] + [(4, 0, 1, 512), (4, 32, 1, 514)]

            def nb_evac(c, pbf, b, gmi=gmi):
                S.add("dve", lambda e: e.tensor_tensor(out=hfN[:, c, 0:2], in0=pbf[:, 512:516:2], in1=gm[gmi][:, c, 0:2], op=ALU.mult),
                      reads=[bk(b), "gm%d" % gmi], writes=[("hfN", c)])

            tm_norm_T(xts, hn, ss, rt, G_FFN, hfT, "hfT", banks, 512, layout, evacs=nb_evac)
            for jj in range(44):
                sl, sk = us_rot.next()
                S.add("pool", lambda e, sl=sl, jj=jj: e.dma_start(out=sl, in_=wup_b[jj].rearrange("p (a b) -> p a b", b=128)), writes=[sk], dma=True)
                ba, bb, bn = banks.next(), banks.next(), banks.next()
                for (bm, so, noff) in ((ba, 0, 0), (bb, 16, 2)):
                    for k in range(DC):
                        S.add("pe", lambda e, k=k, bm=bm, so=so, sl=sl: e.matmul(bank(bm), lhsT=sl[:, so + k, :], rhs=hfT[:, k, :],
                                                                               start=(k == 0), stop=(k == DC - 1)),
                              reads=[("hfT", k), sk], writes=[bk(bm)])
                    for k in range(DC):
                        S.add("pe", lambda e, k=k, so=so, sl=sl, noff=noff: e.matmul(bank(bn, w=2, off=noff), lhsT=sl[:, so + k, :], rhs=hfN[:, k, :],
                                                                                   start=(k == 0), stop=(k == DC - 1)),
                              reads=[("hfN", k), sk], writes=[bk(bn)])
                v = jj % 2
                for (bm, dst, dk, ch, noff) in ((ba, acv[v], ("acv", v), jj, 0), (bb, bcv[v], ("bcv", v), 44 + jj, 2)):
                    w0, w1, w2 = spc(CONVW + ch * 3 + 0), spc(CONVW + ch * 3 + 1), spc(CONVW + ch * 3 + 2)
                    S.add("act", lambda e, bm=bm, dst=dst, ch=ch, w1=w1: e.activation(
                        out=dst, in_=bank(bm), func=AF.Identity, scale=w1, bias=spc(CONVB + ch)),
                        reads=[bk(bm), "smallp"], writes=[dk])
                    S.add("dve", lambda e, bm=bm, dst=dst, w0=w0: e.scalar_tensor_tensor(
                        out=dst[:, 1:512], in0=bank(bm, w=511, off=0), scalar=w0, in1=dst[:, 1:512], op0=ALU.mult, op1=ALU.add),
                        reads=[bk(bm), "smallp", dk], writes=[dk])
                    S.add("dve", lambda e, dst=dst, w0=w0, noff=noff: e.scalar_tensor_tensor(
                        out=dst[:, 0:1], in0=bank(bn, w=1, off=noff), scalar=w0, in1=dst[:, 0:1], op0=ALU.mult, op1=ALU.add),
                        reads=[bk(bn), "smallp", dk], writes=[dk])
                    S.add("dve", lambda e, bm=bm, dst=dst, w2=w2: e.scalar_tensor_tensor(
                        out=dst[:, 0:511], in0=bank(bm, w=511, off=1), scalar=w2, in1=dst[:, 0:511], op0=ALU.mult, op1=ALU.add),
                        reads=[bk(bm), "smallp", dk], writes=[dk])
                    S.add("dve", lambda e, dst=dst, w2=w2, noff=noff: e.scalar_tensor_tensor(
                        out=dst[:, 511:512], in0=bank(bn, w=1, off=noff + 1), scalar=w2, in1=dst[:, 511:512], op0=ALU.mult, op1=ALU.add),
                        reads=[bk(bn), "smallp", dk], writes=[dk])
                S.add("act", lambda e, v=v: e.activation(out=sav[v], in_=acv[v], func=AF.Silu), reads=[("acv", v)], writes=[("sav", v)])
                S.add("dve", lambda e, v=v, jj=jj: e.tensor_tensor(out=gT[:, jj, :], in0=sav[v], in1=bcv[v], op=ALU.mult),
                      reads=[("sav", v), ("bcv", v)], writes=[("gT", jj)])
            for cg in range(4):
                accb = [banks.next() for _ in range(4)]
                for s4 in range(4):
                    sl, sk = ds_rot.next()
                    S.add("pool", lambda e, sl=sl, s4=s4, cg=cg: e.dma_start(out=sl, in_=wdn_b[cg * 4 + s4].rearrange("p (a b) -> p a b", b=512)), writes=[sk], dma=True)
                    for i in range(4):
                        for kk in range(11):
                            kc = s4 * 11 + kk
                            S.add("pe", lambda e, i=i, kk=kk, kc=kc, sl=sl, b=accb[i]: e.matmul(
                                bank(b), lhsT=gT[:, kc, i * 128:(i + 1) * 128], rhs=sl[:, kk, :],
                                start=(kc == 0), stop=(kc == 43)), reads=[("gT", kc), sk], writes=[bk(accb[i])])
                for i in range(4):
                    S.add("dve", lambda e, i=i, b=accb[i], cg=cg: e.tensor_tensor(
                        out=xres[:, i, cg * 512:(cg + 1) * 512], in0=bank(b), in1=xres[:, i, cg * 512:(cg + 1) * 512], op=ALU.add),
                        reads=[bk(accb[i]), ("xres", i)], writes=[("xres", i)])
            if j == 0:
                stop_here("p5b0", [("d_hfT", hfT.rearrange("p a b -> p (a b)")), ("d_gT", gT.rearrange("p a b -> p (a b)")), ("d_acv", acv[1]),
                                   ("d_bcv", bcv[1]), ("d_x3", xres.rearrange("p a b -> p (a b)"))] if dbg else [])
            S.add("dve", lambda e: e.memset(ss[:, 0:8], 0.0), writes=["ss"])
            for i in range(4):
                S.add("act", lambda e, i=i: e.activation(out=hn[:, i, :], in_=xres[:, i, :], func=AF.Square, accum_out=ss[:, i:i + 1]),
                      reads=[("xres", i), "ss"], writes=[("hn", i), ("ssv", i)])
            S.add("act", lambda e: e.activation(out=rt[:, 0:4], in_=ss[:, 0:4], func=AF.Sqrt, scale=1.0 / D, bias=epst),
                  reads=[("ssv", i) for i in range(4)] + ["eps", "ss"], writes=[("rt", 0)])
            S.add("dve", lambda e: e.reciprocal(out=rt[:, 4:8], in_=rt[:, 0:4]), reads=[("rt", 0)], writes=[("rs", 0)])
            for i in range(4):
                S.add("dve", lambda e, i=i: e.scalar_tensor_tensor(out=xres[:, i, :], in0=xres[:, i, :], scalar=rt[:, 4 + i:5 + i], in1=gfb,
                                                                   op0=ALU.mult, op1=ALU.mult), reads=[("xres", i), ("rs", 0), "gfb"], writes=[("xres", i)])
                S.add("sp", lambda e, i=i, r=r0 + i * 128: e.dma_start(out=out[r:r + 128, :], in_=xres[:, i, :]),
                      reads=[("xres", i)], writes=[("out", r0, i)], dma=True)
        S.barrier()
        S.finalize(ctx)
        build_program.peak = A.peak
    return nc


def _rope_tables(pos_list):
    f32 = np.float32
    half = 32
    inv = np.power(f32(10000.0), (-2.0 * np.arange(half, dtype=f32) / f32(64)).astype(f32)).astype(f32)
    ang = (pos_list.astype(f32)[None, :] * inv[:, None]).astype(f32)
    c = np.cos(ang).astype(f32)
    s = np.sin(ang).astype(f32)
    C = np.concatenate([c, c], axis=0)
    Sg = np.concatenate([-s, s], axis=0)
    return np.stack([C, Sg], axis=0).astype(f32)


def _prep_core(core, inputs, shared):
    b, half = core // 2, core % 2
    own = np.arange(half * NQ, half * NQ + NQ)
    if half == 0:
        other = np.arange(NQ, SEQ)
    else:
        other = np.concatenate([[NQ - 1], np.arange(0, NQ - 1)])
    order = np.concatenate([own, other])
    x = inputs["x"]
    m = dict(shared)
    m["xkv"] = np.ascontiguousarray(x[b][order])
    m["mem"] = np.ascontiguousarray(inputs["mem"][b])
    tpos = order.astype(np.float32)
    rows = (order // 64).astype(np.float32)
    cols = (order % 64).astype(np.float32)
    m["rope1"] = _rope_tables(tpos)
    r_rows = _rope_tables(rows)
    r_cols = _rope_tables(cols)
    m["rope2"] = np.ascontiguousarray(np.concatenate([r_rows, r_cols], axis=1))
    sp = shared["smallp"].copy()
    sp[:, 74] = 0.0 if half == 0 else 1.0
    sp[:, 75] = 1.0 if half == 0 else 0.0
    m["smallp"] = sp
    return m


def _shared_inputs(inputs):
    f32 = np.float32
    sh = {}
    for k in ("w_in", "w_uq", "w_ukv"):
        sh[k] = np.ascontiguousarray(inputs[k][0], dtype=f32)

    def kview(w):
        w = np.asarray(w, f32)
        return w.reshape(w.shape[0] // P, P, w.shape[1]).transpose(1, 0, 2)

    wi = kview(inputs["w_in"][0])
    sh["wqB_d"] = np.ascontiguousarray(wi[:, :, 1088:2112].reshape(P, DC, 8, 128).transpose(2, 0, 1, 3))
    wg = kview(inputs["w_gate"][0])
    gsl = np.empty((DC, P, 48, 128), f32)
    gsl[:, :, 0:16] = wg[:, :, 0:D].reshape(P, DC, DC, 128).transpose(2, 0, 1, 3)
    gsl[:, :, 16:32] = wg[:, :, D:2 * D].reshape(P, DC, DC, 128).transpose(2, 0, 1, 3)
    gsl[:, :, 32:40] = kview(inputs["w_o_mla"][0]).reshape(P, 8, DC, 128).transpose(2, 0, 1, 3)
    gsl[:, :, 40:48] = kview(inputs["w_o_gqa"][0]).reshape(P, 8, DC, 128).transpose(2, 0, 1, 3)
    sh["gsl_d"] = gsl
    sh["wout_d"] = np.ascontiguousarray(kview(inputs["w_out"][0]).reshape(P, DC, 8, 256).transpose(2, 0, 1, 3))
    sh["wxq_d"] = np.ascontiguousarray(kview(inputs["w_xq"][0]).reshape(P, DC, 2, 256).transpose(2, 0, 1, 3))
    sh["wxkv_d"] = np.ascontiguousarray(kview(inputs["w_xkv"][0]).reshape(P, DC, 4, 256).transpose(2, 0, 1, 3))
    sh["wxo_d"] = np.ascontiguousarray(kview(inputs["w_xo"][0]).reshape(P, 4, 4, 512).transpose(2, 0, 1, 3))
    sh["wup_d"] = np.ascontiguousarray(
        kview(inputs["w_up"][0]).reshape(P, DC, 2, 44, 128).transpose(3, 0, 2, 1, 4).reshape(44, P, 32, 128))
    sh["wdn_d"] = np.ascontiguousarray(kview(inputs["w_down"][0]).reshape(P, 4, 11, 4, 512).transpose(3, 1, 0, 2, 4))
    sp = np.zeros((P, 512), f32)

    def cols(v, n):
        return np.asarray(v, f32).reshape(n, P).T

    sp[:, 0:16] = cols(inputs["norm_mix"][0], 16)
    sp[:, 16:32] = cols(inputs["norm_cross"][0], 16)
    sp[:, 32:48] = cols(inputs["norm_mem"][0], 16)
    sp[:, 48:64] = cols(inputs["norm_ffn"][0], 16)
    sp[:, 64:68] = cols(inputs["mla_q_norm"][0], 4)
    sp[:, 68:72] = cols(inputs["mla_kv_norm"][0], 4)
    sp[:, 72:73] = cols(inputs["gqa_q_norm"][0], 1)
    sp[:, 73:74] = cols(inputs["gqa_k_norm"][0], 1)
    sp[:, 80:112] = cols(inputs["b_gate"][0], 32)
    sp[:, 112:200] = cols(inputs["conv_b"][0], 88)
    cw = np.asarray(inputs["conv_w"][0], f32)
    sp[:, 200:464] = cw.reshape(3, 88, P).transpose(2, 1, 0).reshape(P, 264)
    sh["smallp"] = sp
    sh["gfin"] = np.asarray(inputs["norm_final"], f32).reshape(1, D)
    sh["ident"] = np.eye(P, dtype=f32)
    pm = np.zeros((P, P), f32)
    for f in range(P):
        blk, r = f // 64, f % 64
        pm[blk * 64 + (r + 32) % 64, f] = 1.0
    sh["perm"] = pm
    return sh


_CACHE = {}


def kernel(**inputs):
    inputs = {k: np.asarray(v) for k, v in inputs.items()}
    if "nc" not in _CACHE:
        _CACHE["nc"] = build_program()
    nc = _CACHE["nc"]
    shared = _shared_inputs(inputs)
    in_maps = [_prep_core(c, inputs, shared) for c in range(8)]
    res = run_bass_kernel_spmd(nc, in_maps, core_ids=list(range(8)))
    outp = np.empty((4, SEQ, D), np.float32)
    for c in range(8):
        b, half = c // 2, c % 2
        outp[b, half * NQ:(half + 1) * NQ, :] = res.results[c]["out"]
    return outp
```

```python
from contextlib import ExitStack
from concourse.bass_utils import run_bass_kernel_spmd
import numpy as np
import concourse.bass as bass
import concourse.mybir as mybir

F32 = mybir.dt.float32
BF16 = mybir.dt.bfloat16
ALU = mybir.AluOpType
AF = mybir.ActivationFunctionType
AX = mybir.AxisListType

EPOCH = 30000
DMA_RING = {"sp": 8, "pool": 6, "act": 4}


class Op:
    __slots__ = ("eng", "idx", "emit", "deps", "is_dma", "signal", "dma_n", "tick", "barrier")

    def __init__(self, eng, idx, emit, is_dma):
        self.eng = eng
        self.idx = idx
        self.emit = emit
        self.deps = set()
        self.is_dma = is_dma
        self.signal = is_dma
        self.dma_n = -1
        self.tick = -1
        self.barrier = False


class _Recorder:
    def __init__(self):
        self.call = None

    def __getattr__(self, name):
        def f(*args, **kwargs):
            assert self.call is None, "emit closure must make exactly one engine call"
            self.call = (name, args, kwargs)
            return self
        return f


class Sched:
    STREAMS = ("pe", "act", "dve", "pool", "sp")

    def __init__(self, nc):
        self.nc = nc
        self.ops = {s: [] for s in self.STREAMS}
        self.res = {}
        self.dma_count = {s: 0 for s in self.STREAMS}
        self._bar_pos = {}
        self.stopped = False

    def add(self, eng, emit, reads=(), writes=(), dma=False):
        if self.stopped:
            return None
        lst = self.ops[eng]
        if emit is not None:
            rec = _Recorder()
            emit(rec)
            name_, args_, kwargs_ = rec.call
            emit = (lambda e, name_=name_, args_=args_, kwargs_=kwargs_: getattr(e, name_)(*args_, **kwargs_))
        op = Op(eng, len(lst), emit, dma)
        if dma:
            op.dma_n = self.dma_count[eng]
            self.dma_count[eng] += 1
        for k in reads:
            st = self.res.get(k)
            if st is None:
                st = self.res[k] = [None, []]
            if st[0] is not None:
                op.deps.add(st[0])
        for k in writes:
            st = self.res.get(k)
            if st is None:
                st = self.res[k] = [None, []]
            if st[0] is not None:
                op.deps.add(st[0])
            for r in st[1]:
                op.deps.add(r)
        for k in reads:
            self.res[k][1].append(op)
        for k in writes:
            st = self.res[k]
            st[0] = op
            st[1] = []
        op.deps.discard(op)
        lst.append(op)
        return op

    def barrier(self):
        if self.stopped:
            return
        lasts = []
        for s in self.STREAMS:
            for o in reversed(self.ops[s]):
                if o.emit is not None and not o.is_dma:
                    lasts.append(o)
                    break
        pend = []
        for s in self.STREAMS:
            for o in self.ops[s][self._bar_pos.get(s, 0):]:
                if o.is_dma:
                    pend.append(o)
        for s in self.STREAMS:
            op = self.add(s, None)
            op.barrier = True
            op.deps.update(lasts)
            op.deps.update(pend)
        for s in self.STREAMS:
            self._bar_pos[s] = len(self.ops[s])
        self.res = {}

    def finalize(self, ctx):
        nc = self.nc
        for s in self.STREAMS:
            for op in self.ops[s]:
                for d in op.deps:
                    if not d.is_dma:
                        if d.eng == op.eng:
                            if d.eng == "pe" and not op.barrier:
                                continue
                            if (op.idx - d.idx) > 2 and not op.barrier:
                                continue
                        d.signal = True
        self.prog_sems = {}
        for s in self.STREAMS:
            t = 0
            for op in self.ops[s]:
                if op.signal and not op.is_dma and op.emit is not None:
                    op.tick = t
                    t += 1
            n_ep = (t + EPOCH - 1) // EPOCH
            self.prog_sems[s] = [ctx.enter_context(nc.semaphore(f"pg_{s}_{e}")) for e in range(max(n_ep, 1))]
        self.dma_sems = {}
        for s in self.STREAMS:
            if self.dma_count[s]:
                self.dma_sems[s] = [ctx.enter_context(nc.semaphore(f"dq_{s}_{i}")) for i in range(DMA_RING[s])]

        def target(d):
            if d.is_dma:
                R = DMA_RING[d.eng]
                return self.dma_sems[d.eng][d.dma_n % R], 16 * (d.dma_n // R + 1)
            return self.prog_sems[d.eng][d.tick // EPOCH], d.tick % EPOCH + 1

        sched = self
        ctx.enter_context(nc.allow_non_contiguous_dma(reason="single-column halo transfers"))
        block = ctx.enter_context(nc.Block())

        def run_stream(s, eng):
            waited = {}

            def wait(sem, val):
                k = id(sem)
                if waited.get(k, 0) >= val:
                    return
                waited[k] = val
                eng.wait_ge(sem, val)

            for op in sched.ops[s]:
                for d in sorted(op.deps, key=lambda o: (o.eng, o.idx)):
                    if not d.is_dma:
                        if d.emit is None:
                            continue
                        if d.eng == s:
                            if s == "pe" and not op.barrier:
                                continue
                            if (op.idx - d.idx) > 2 and not op.barrier:
                                continue
                        if d.tick < 0:
                            continue
                    sem, val = target(d)
                    wait(sem, val)
                if op.emit is None:
                    continue
                if op.is_dma:
                    R = DMA_RING[s]
                    if op.dma_n >= R:
                        wait(sched.dma_sems[s][op.dma_n % R], 16 * (op.dma_n // R))
                    ins = op.emit(eng)
                    ins.then_inc(sched.dma_sems[s][op.dma_n % R], 16)
                else:
                    ins = op.emit(eng)
                    if op.signal:
                        ins.then_inc(sched.prog_sems[s][op.tick // EPOCH], 1)

        @block.tensor
        def _(e):
            run_stream("pe", e)

        @block.scalar
        def _(e):
            run_stream("act", e)

        @block.vector
        def _(e):
            run_stream("dve", e)

        @block.gpsimd
        def _(e):
            run_stream("pool", e)

        @block.sync
        def _(e):
            run_stream("sp", e)


P = 128
D = 2048
DC = 16
SEQ = 4096
NQ = 2048
NQH = NQ + 1
N_IN = 2624
DFF = 5632
EPS = 1e-6
SBUF_ELEMS = 104000

HALO_W = 2
QBLOCKS = [(0, 512), (512, 512), (1024, 512), (1536, 512), (2048, HALO_W)]


def _tiles_of(w):
    if w == 512:
        return [(i * 128, 128) for i in range(4)]
    return [(0, w)]


class Arena:
    def __init__(self, big, cap):
        self.big = big
        self.cap = cap
        self.off = 0
        self.peak = 0

    def mark(self):
        return self.off

    def release(self, m):
        self.off = m

    def __call__(self, shape, dtype, parts=None):
        parts = shape[0] if parts is None else parts
        n = 1
        for s_ in shape[1:]:
            n *= s_
        nb = n * 2 if dtype == F32 else n
        off = (self.off + 15) // 16 * 16
        assert off + nb <= self.cap, f"SBUF arena overflow: {off + nb} > {self.cap}"
        self.off = off + nb
        self.peak = max(self.peak, self.off)
        v = self.big[0:parts, off:off + nb]
        if dtype == F32:
            v = v.bitcast(F32)
        if len(shape) == 3:
            v = v.rearrange("p (a b) -> p a b", b=shape[2])
        elif len(shape) == 4:
            v = v.rearrange("p (a b c) -> p a b c", b=shape[2], c=shape[3])
        return v


class Rot:
    def __init__(self, items):
        self.items = list(items)
        self.i = 0

    def next(self):
        v = self.items[self.i % len(self.items)]
        self.i += 1
        return v


def build_program(dbg=False, stop_after=None, skip_to=None, p4_limit=None):
    nc = bass.Bass("TRN2", target_bir_lowering=False)

    def din(name, shape, dt=F32):
        return nc.dram_tensor(name, list(shape), dt, kind="ExternalInput").ap()

    xkv = din("xkv", [SEQ, D])
    mem = din("mem", [256, D])
    w_in = din("w_in", [D, N_IN])
    w_uq = din("w_uq", [512, 1536])
    w_ukv = din("w_ukv", [512, 2048])
    wqB_d = din("wqB_d", [8, P, DC, 128])
    gsl_d = din("gsl_d", [DC, P, 48, 128])
    wout_d = din("wout_d", [8, P, DC, 256])
    wxq_d = din("wxq_d", [2, P, DC, 256])
    wxkv_d = din("wxkv_d", [4, P, DC, 256])
    wxo_d = din("wxo_d", [4, P, 4, 512])
    wup_d = din("wup_d", [44, P, 32, 128])
    wdn_d = din("wdn_d", [4, 4, P, 11, 512])
    smallp = din("smallp", [P, 512])
    gfin = din("gfin", [1, D])
    ident_d = din("ident", [P, P])
    perm_d = din("perm", [P, P])
    rope1 = din("rope1", [2, 64, SEQ])
    rope2 = din("rope2", [2, P, SEQ])
    out = nc.dram_tensor("out", [NQ, D], F32, kind="ExternalOutput").ap()
    gsl_b = nc.dram_tensor("gsl_b", [DC, P, 48 * 128], BF16).ap()
    wout_b = nc.dram_tensor("wout_b", [8, P, DC * 256], BF16).ap()
    wxq_b = nc.dram_tensor("wxq_b", [2, P, DC * 256], BF16).ap()
    wxo_b = nc.dram_tensor("wxo_b", [4, P, 4 * 512], BF16).ap()
    wup_b = nc.dram_tensor("wup_b", [44, P, 32 * 128], BF16).ap()
    wdn_b = nc.dram_tensor("wdn_b", [16, P, 11 * 512], BF16).ap()
    skind = "ExternalOutput" if dbg else "Internal"
    yA_d = nc.dram_tensor("yA_d", [8, P, NQH + 7], BF16, kind=skind).ap()
    yB_d = nc.dram_tensor("yB_d", [8, P, NQH + 7], BF16, kind=skind).ap()
    x2_d = nc.dram_tensor("x2_d", [NQ + HALO_W, D], F32, kind=skind).ap()
    dbg_t = {}
    if dbg:
        for nm, shp, dt_ in (("d_ckvT", [P, 4 * SEQ], BF16), ("d_krT", [64, SEQ], BF16), ("d_kBT", [P, 2 * SEQ], BF16),
                             ("d_vB", [P, 32 * 256], BF16), ("d_qBT", [P, 8 * (NQH + 7)], BF16), ("d_cqT", [P, 4 * (NQH + 7)], BF16),
                             ("d_hT", [P, 16 * 512], BF16), ("d_hn", [P, 4 * D], BF16), ("d_rt", [P, 16], F32), ("d_ss", [P, 16], F32),
                             ("d_tmp", [P, 2048], F32), ("d_hfT", [P, 16 * 512], BF16), ("d_gT", [P, 44 * 512], BF16), ("d_acv", [P, 512], F32),
                             ("d_bcv", [P, 512], F32), ("d_x3", [P, 4 * D], F32), ("d_sq", [P, 2048], BF16), ("d_rstd", [P, 512], F32), ("d_xt", [P, D], F32)):
            dbg_t[nm] = nc.dram_tensor(nm, shp, dt_, kind="ExternalOutput").ap()

    def wview(w):
        return w.rearrange("(c p) n -> p c n", p=P)

    w_in_v, w_uq_v, w_ukv_v = wview(w_in), wview(w_uq), wview(w_ukv)

    ctx = ExitStack()
    with ctx:
        S = Sched(nc)
        big = nc.alloc_sbuf_tensor("arena", [P, SBUF_ELEMS], BF16)
        A = Arena(big, SBUF_ELEMS)
        psall = ctx.enter_context(nc.psum_tensor("psall", [P, 4096], F32))

        def bank(i, parts=P, w=512, off=0):
            return psall[0:parts, i * 512 + off: i * 512 + off + w]

        def bankbf(i, parts=P):
            return psall[0:parts, i * 512:(i + 1) * 512].bitcast(BF16)

        def bk(i):
            return ("pb", i)

        ident = A([P, P], BF16)
        perm = A([P, P], BF16)
        ones = A([P, P], BF16)
        sp_ = A([P, 512], F32)
        epst = A([P, 1], F32)
        S.add("pool", lambda e: e.dma_start(out=ident, in_=ident_d), writes=["ident"], dma=True)
        S.add("pool", lambda e: e.dma_start(out=perm, in_=perm_d), writes=["perm"], dma=True)
        S.add("sp", lambda e: e.dma_start(out=sp_, in_=smallp), writes=["smallp"], dma=True)
        S.add("dve", lambda e: e.memset(ones, 1.0), writes=["ones"])
        S.add("dve", lambda e: e.memset(epst, EPS), writes=["eps"])
        G_MIX, G_CROSS, G_MEM, G_FFN = 0, 16, 32, 48
        G_MLAQ, G_MLAKV, G_GQ, G_GK = 64, 68, 72, 73
        MASKL, MASKR = 74, 75
        BGATE = 80
        CONVB = 112
        CONVW = 200

        def spc(c, n=1):
            return sp_[:, c:c + n]

        CONST_R = ["ident", "perm", "ones", "smallp", "eps"]
        S.barrier()

        def stop_here(name, dumps=()):
            if stop_after != name:
                return
            S.barrier()
            for (nm, ap2d) in dumps:
                S.add("sp", lambda e, nm=nm, ap2d=ap2d: e.dma_start(out=dbg_t[nm], in_=ap2d), dma=True)
            S.barrier()
            S.stopped = True

        tog = {"i": 0}

        def evac_engine():
            tog["i"] += 1
            return "dve" if tog["i"] % 2 else "act"

        def copy_op(eng, out_ap, in_ap, reads, writes):
            if eng == "act":
                S.add("act", lambda e: e.copy(out=out_ap, in_=in_ap), reads=reads, writes=writes)
            else:
                S.add(eng, lambda e: e.tensor_copy(out=out_ap, in_=in_ap), reads=reads, writes=writes)

        def tm_norm_T(xtiles, hn, ss, rt, gain_col, dstT, dst_key, banks, ncols_total, layout, scale_eng="act", evacs=None):
            nt = len(xtiles)
            S.add("dve", lambda e: e.memset(ss[:, 0:8], 0.0), writes=["ss"])
            for g0 in range(0, nt, 2):
                grp = list(range(g0, min(g0 + 2, nt)))
                aps = {}
                for i in grp:
                    aps[i] = xtiles[i][0]()
                pmax = max(xtiles[i][2] for i in grp)
                for i in grp:
                    xa, xk, npp = aps[i], xtiles[i][1], xtiles[i][2]
                    S.add("act", lambda e, xa=xa, i=i, npp=npp: e.activation(
                        out=hn[0:npp, i, :], in_=xa, func=AF.Square, accum_out=ss[0:npp, i:i + 1]),
                        reads=[xk, "ss"], writes=[("hn", i), ("ssv", i)])
                c0, c1 = grp[0], grp[-1] + 1
                S.add("act", lambda e, c0=c0, c1=c1, pmax=pmax: e.activation(
                    out=rt[0:pmax, c0:c1], in_=ss[0:pmax, c0:c1], func=AF.Sqrt, scale=1.0 / D, bias=epst[0:pmax, :]),
                    reads=[("ssv", i) for i in grp] + ["eps", "ss"], writes=[("rt", g0)])
                S.add("dve", lambda e, c0=c0, c1=c1, pmax=pmax: e.reciprocal(out=rt[0:pmax, 8 + c0:8 + c1], in_=rt[0:pmax, c0:c1]),
                      reads=[("rt", g0)], writes=[("rs", g0)])
                for i in grp:
                    xa, xk, npp = aps[i], xtiles[i][1], xtiles[i][2]
                    if scale_eng == "act":
                        S.add("act", lambda e, xa=xa, i=i, npp=npp: e.activation(
                            out=hn[0:npp, i, :], in_=xa, func=AF.Identity, scale=rt[0:npp, 8 + i:9 + i]),
                            reads=[xk, ("rs", g0)], writes=[("hn", i)])
                    else:
                        S.add("dve", lambda e, xa=xa, i=i, npp=npp: e.tensor_scalar_mul(
                            out=hn[0:npp, i, :], in0=xa, scalar1=rt[0:npp, 8 + i:9 + i]),
                            reads=[xk, ("rs", g0)], writes=[("hn", i)])
            for c in range(DC):
                b = banks.next()
                pbf = bankbf(b)
                for (ti, p0, npp, dcol) in layout:
                    S.add("pe", lambda e, ti=ti, p0=p0, npp=npp, dcol=dcol, c=c, pbf=pbf: e.transpose(
                        out=pbf[:, dcol:dcol + npp], in_=hn[p0:p0 + npp, ti, c * 128:(c + 1) * 128],
                        identity=ident[p0:p0 + npp, p0:p0 + npp]),
                        reads=[("hn", ti), "ident"], writes=[bk(b)])
                S.add("dve", lambda e, c=c, pbf=pbf: e.tensor_scalar_mul(
                    out=dstT[:, c, 0:ncols_total], in0=pbf[:, 0:ncols_total], scalar1=spc(gain_col + c)), reads=[bk(b), "smallp"], writes=[(dst_key, c)])
                if evacs is not None:
                    evacs(c, pbf, b)

        def fm_rmsnorm(src_banks, nfeat, W, gain_col, tmp, sq, rstd, dsts, dst_keys, banks, parts=P, coff=0, rkey="fm_rstd"):
            n = len(src_banks)
            for c, b in enumerate(src_banks):
                S.add("act", lambda e, c=c, b=b: e.copy(out=tmp[:, coff + c, 0:W], in_=bank(b, w=W)),
                      reads=[bk(b)], writes=[("fm_tmp", coff + c)])
                S.add("dve", lambda e, c=c: e.tensor_tensor(out=sq[:, coff + c, 0:W], in0=tmp[:, coff + c, 0:W], in1=tmp[:, coff + c, 0:W],
                                                            op=ALU.mult), reads=[("fm_tmp", coff + c)], writes=[("fm_sq", coff + c)])
            bs = banks.next()
            for c in range(n):
                S.add("pe", lambda e, c=c, bs=bs: e.matmul(bank(bs, w=W), lhsT=ones, rhs=sq[:, coff + c, 0:W],
                                                          start=(c == 0), stop=(c == n - 1)),
                      reads=[("fm_sq", coff + c), "ones"], writes=[bk(bs)])
            S.add("act", lambda e, bs=bs: e.activation(out=rstd[:, 0:W], in_=bank(bs, w=W), func=AF.Sqrt,
                                                      scale=1.0 / nfeat, bias=epst), reads=[bk(bs), "eps"], writes=[rkey])
            S.add("dve", lambda e: e.reciprocal(out=rstd[:, 0:W], in_=rstd[:, 0:W]), reads=[rkey], writes=[rkey])
            for c in range(n):
                S.add("dve", lambda e, c=c: e.scalar_tensor_tensor(
                    out=dsts[c], in0=tmp[:, coff + c, 0:W], scalar=spc(gain_col + c), in1=rstd[:, 0:W],
                    op0=ALU.mult, op1=ALU.mult), reads=[("fm_tmp", coff + c), rkey, "smallp"], writes=[dst_keys[c]])

        def rope_fm(src_bf, src_key, np_, W, Ct, St, tab_keys, t1, t2, dst, dst_key, banks, tkey=0):
            b = banks.next()
            S.add("pe", lambda e, b=b: e.matmul(bank(b, parts=np_, w=W), lhsT=perm[0:np_, 0:np_], rhs=src_bf,
                                                start=True, stop=True), reads=[src_key, "perm"], writes=[bk(b)])
            S.add("dve", lambda e: e.tensor_tensor(out=t1[0:np_, 0:W], in0=src_bf, in1=Ct, op=ALU.mult),
                  reads=[src_key] + list(tab_keys), writes=[("rp_t1", tkey)])
            S.add("dve", lambda e, b=b: e.tensor_tensor(out=t2[0:np_, 0:W], in0=bank(b, parts=np_, w=W), in1=St, op=ALU.mult),
                  reads=[bk(b)] + list(tab_keys), writes=[("rp_t2", tkey)])
            S.add("dve", lambda e: e.tensor_tensor(out=dst, in0=t1[0:np_, 0:W], in1=t2[0:np_, 0:W], op=ALU.add),
                  reads=[("rp_t1", tkey), ("rp_t2", tkey)], writes=[dst_key])

        def load_x_tiles(xt_rot, row0, W):
            res = []
            for (t0, npp) in _tiles_of(W):
                xa, xk = xt_rot.next()

                def ld(xa=xa, xk=xk, npp=npp, r=row0 + t0):
                    S.add("sp", lambda e: e.dma_start(out=xa[0:npp, :], in_=xkv[r:r + npp, :]), writes=[xk], dma=True)
                    return xa[0:npp, :]
                res.append((ld, xk, npp))
            return res

        def std_layout(W):
            return [(i, 0, npp, t0) for i, (t0, npp) in enumerate(_tiles_of(W))]

        class AttnPipe:
            def __init__(self, la=3):
                self.tasks = []
                self.la = la

            def add(self, qk, ex, pv, end=None, pre=None):
                self.tasks.append((qk, ex, pv, end, pre))

            def run(self):
                n = len(self.tasks)
                for i in range(n + self.la):
                    if i < n:
                        if self.tasks[i][4] is not None:
                            self.tasks[i][4]()
                        self.tasks[i][0]()
                    j = i - self.la
                    if j >= 0:
                        self.tasks[j][1]()
                        self.tasks[j][2]()
                        if self.tasks[j][3] is not None:
                            self.tasks[j][3]()

        def attention_tasks(pipe, nkt, W, K1, K2, Q1, Q2, Vfn, scale, wbanks, pts, acc_o, acc_s, rc, finish,
                            rk, pre=None):
            for kt in range(nkt):
                b = wbanks.next()
                pt, pk = pts.next()

                def qk(kt=kt, b=b):
                    S.add("pe", lambda e: e.matmul(bank(b, w=W), lhsT=K1(kt), rhs=Q1, start=True, stop=(K2 is None)),
                          reads=rk["k"] + rk["q"], writes=[bk(b)])
                    if K2 is not None:
                        S.add("pe", lambda e: e.matmul(bank(b, w=W), lhsT=K2(kt), rhs=Q2, start=False, stop=True),
                              reads=rk["k"] + rk["q"], writes=[bk(b)])

                def ex(b=b, pt=pt, pk=pk):
                    S.add("act", lambda e: e.activation(out=pt[:, 0:W], in_=bank(b, w=W), func=AF.Exp, scale=scale),
                          reads=[bk(b)], writes=[pk])

                def pv(kt=kt, pt=pt, pk=pk):
                    S.add("pe", lambda e: e.matmul(bank(acc_o, w=W), lhsT=Vfn(kt), rhs=pt[:, 0:W],
                                                   start=(kt == 0), stop=(kt == nkt - 1)),
                          reads=rk["v"] + [pk], writes=[bk(acc_o)])
                    S.add("pe", lambda e: e.matmul(bank(acc_s, w=W), lhsT=ones, rhs=pt[:, 0:W],
                                                   start=(kt == 0), stop=(kt == nkt - 1)),
                          reads=["ones", pk], writes=[bk(acc_s)])

                end = None
                if kt == nkt - 1:
                    def end():
                        S.add("dve", lambda e: e.reciprocal(out=rc[:, 0:W], in_=bank(acc_s, w=W)),
                              reads=[bk(acc_s)], writes=["rc"])
                        finish()
                pipe.add(qk, ex, pv, end, pre if kt == 0 else None)

        m_att = A.mark()
        ckvT = A([P, 4, SEQ], BF16)
        krT = A([64, SEQ], BF16)
        m_gqa = A.mark()
        kBT = A([P, 2, SEQ], BF16)
        vB = A([P, 32, 256], BF16)
        m_p1 = A.mark()

        if skip_to is not None:
            S.stopped = True
        winkv = A([P, DC, 1088], BF16)
        xt = [A([P, D], F32) for _ in range(2)]
        xt_rot = Rot([(xt[i], ("xt", i)) for i in range(2)])
        hn = A([P, 4, D], BF16)
        hT = A([P, DC, 512], BF16)
        ss = A([P, 16], F32)
        rt = A([P, 16], F32)
        tabs = [[A([P, 512], F32) for _ in range(4)] for _ in range(2)]
        fm_tmp = A([P, 4, 512], F32)
        fm_sq = A([P, 4, 512], BF16)
        fm_rstd = A([P, 512], F32)
        nrm_bf = A([P, 512], BF16)
        rp_t1 = A([P, 512], F32)
        rp_t2 = A([P, 512], F32)
        banks = Rot(range(8))

        for q4 in range(4):
            S.add("pool", lambda e, q4=q4: e.dma_start(out=winkv[:, q4 * 4:(q4 + 1) * 4, 0:576],
                                                     in_=w_in_v[:, q4 * 4:(q4 + 1) * 4, 512:1088]),
                  writes=[("winkv", q4)], dma=True)
            S.add("pool", lambda e, q4=q4: e.dma_start(out=winkv[:, q4 * 4:(q4 + 1) * 4, 576:1088],
                                                     in_=w_in_v[:, q4 * 4:(q4 + 1) * 4, 2112:2624]),
                  writes=[("winkv", q4)], dma=True)
        WKV_R = [("winkv", q4) for q4 in range(4)]

        def load_tabs(tset, c0, W, need1=True, need2=True):
            tb = tabs[tset]
            if need1:
                S.add("sp", lambda e: e.dma_start(out=tb[0][0:64, 0:W], in_=rope1[0, :, c0:c0 + W]), writes=[("tab", tset, 0)], dma=True)
                S.add("sp", lambda e: e.dma_start(out=tb[1][0:64, 0:W], in_=rope1[1, :, c0:c0 + W]), writes=[("tab", tset, 1)], dma=True)
            if need2:
                S.add("sp", lambda e: e.dma_start(out=tb[2][:, 0:W], in_=rope2[0, :, c0:c0 + W]), writes=[("tab", tset, 2)], dma=True)
                S.add("sp", lambda e: e.dma_start(out=tb[3][:, 0:W], in_=rope2[1, :, c0:c0 + W]), writes=[("tab", tset, 3)], dma=True)

        for kb in range(8):
            c0 = kb * 512
            tset = kb % 2
            load_tabs(tset, c0, 512)
            xts = load_x_tiles(xt_rot, c0, 512)
            tm_norm_T(xts, hn, ss, rt, G_MIX, hT, "hT", banks, 512, std_layout(512))
            HT_R = [("hT", c) for c in range(DC)]
            cb = [banks.next() for _ in range(4)]
            for cc in range(4):
                for k in range(DC):
                    S.add("pe", lambda e, cc=cc, k=k, b=cb[cc]: e.matmul(
                        bank(b), lhsT=winkv[:, k, cc * 128:(cc + 1) * 128], rhs=hT[:, k, :],
                        start=(k == 0), stop=(k == DC - 1)), reads=[("hT", k), ("winkv", k // 4)], writes=[bk(cb[cc])])
            fm_rmsnorm(cb, 512, 512, G_MLAKV, fm_tmp, fm_sq, fm_rstd,
                       [ckvT[:, cc, c0:c0 + 512] for cc in range(4)], [("ckvT", cc, kb) for cc in range(4)], banks)
            b = banks.next()
            for k in range(DC):
                S.add("pe", lambda e, k=k, b=b: e.matmul(bank(b, parts=64), lhsT=winkv[:, k, 512:576], rhs=hT[:, k, :],
                                                         start=(k == 0), stop=(k == DC - 1)),
                      reads=[("hT", k), ("winkv", k // 4)], writes=[bk(b)])
            S.add("act", lambda e, b=b: e.copy(out=nrm_bf[0:64, :], in_=bank(b, parts=64)), reads=[bk(b)], writes=["nrm_bf"])
            rope_fm(nrm_bf[0:64, :], "nrm_bf", 64, 512, tabs[tset][0][0:64, :], tabs[tset][1][0:64, :], [("tab", tset, 0), ("tab", tset, 1)],
                    rp_t1, rp_t2, krT[:, c0:c0 + 512], ("krT", kb), banks)
            for g in range(2):
                b = banks.next()
                for k in range(DC):
                    S.add("pe", lambda e, k=k, b=b, g=g: e.matmul(
                        bank(b), lhsT=winkv[:, k, 576 + g * 128:576 + (g + 1) * 128], rhs=hT[:, k, :],
                        start=(k == 0), stop=(k == DC - 1)), reads=[("hT", k), ("winkv", k // 4)], writes=[bk(b)])
                fm_rmsnorm([b], 128, 512, G_GK, fm_tmp, fm_sq, fm_rstd, [nrm_bf[:, :]], ["nrm_bf"], banks)
                rope_fm(nrm_bf[:, :], "nrm_bf", P, 512, tabs[tset][2][:, :], tabs[tset][3][:, :], [("tab", tset, 2), ("tab", tset, 3)],
                        rp_t1, rp_t2, kBT[:, g, c0:c0 + 512], ("kBT", g, kb), banks)
            for pr in range(2):
                b = banks.next()
                for tt in range(2):
                    t = pr * 2 + tt
                    for k in range(DC):
                        S.add("pe", lambda e, k=k, b=b, t=t, tt=tt: e.matmul(
                            bank(b, w=256, off=tt * 256), lhsT=hT[:, k, t * 128:(t + 1) * 128], rhs=winkv[:, k, 832:1088],
                            start=(k == 0), stop=(k == DC - 1)), reads=[("hT", k), ("winkv", k // 4)], writes=[bk(b)])
                eng = evac_engine()
                copy_op(eng, vB[:, kb * 4 + pr * 2: kb * 4 + pr * 2 + 2, :].rearrange("p a b -> p (a b)"), bank(b),
                        [bk(b)], [("vB", kb)])
            if kb == 0:
                stop_here("p1b0", [("d_hT", hT.rearrange("p a b -> p (a b)")), ("d_hn", hn.rearrange("p a b -> p (a b)")), ("d_rt", rt), ("d_ss", ss),
                                   ("d_tmp", fm_tmp.rearrange("p a b -> p (a b)")), ("d_sq", fm_sq.rearrange("p a b -> p (a b)")),
                                   ("d_rstd", fm_rstd), ("d_xt", xt[1]), ("d_ckvT", ckvT.rearrange("p a b -> p (a b)")), ("d_krT", krT),
                                   ("d_vB", vB.rearrange("p a b -> p (a b)"))] if dbg else [])
        S.barrier()
        A.release(m_p1)

        stop_here("p1", [("d_ckvT", ckvT.rearrange("p a b -> p (a b)")), ("d_krT", krT), ("d_kBT", kBT.rearrange("p a b -> p (a b)")),
                         ("d_vB", vB.rearrange("p a b -> p (a b)"))] if dbg else [])
        qBT = A([P, 8, NQH + 7], BF16)
        m_p1 = A.mark()
        xt = [A([P, D], F32) for _ in range(2)]
        xt_rot = Rot([(xt[i], ("xt", i)) for i in range(2)])
        hn = A([P, 4, D], BF16)
        hT = A([P, DC, 512], BF16)
        ss = A([P, 16], F32)
        rt = A([P, 16], F32)
        tabs = [[None, None, A([P, 512], F32), A([P, 512], F32)] for _ in range(2)]
        fm_tmp = A([P, 4, 512], F32)
        fm_sq = A([P, 4, 512], BF16)
        fm_rstd = A([P, 512], F32)
        nrm_bf = A([P, 512], BF16)
        rp_t1 = A([P, 512], F32)
        rp_t2 = A([P, 512], F32)
        qslab = [A([P, DC, 128], BF16) for _ in range(2)]
        qs_rot = Rot([(qslab[i], ("qslab", i)) for i in range(2)])
        scr2 = (A([P, 512], F32), A([P, 512], BF16), A([P, 512], F32), A([P, 512], F32))
        banks = Rot(range(8))
        for qi, (q0, W) in enumerate(QBLOCKS):
            tset = qi % 2
            load_tabs(tset, q0, W, need1=False)
            xts = load_x_tiles(xt_rot, q0, W)
            tm_norm_T(xts, hn, ss, rt, G_MIX, hT, "hT", banks, W, std_layout(W))
            for h in range(8):
                sl, sk = qs_rot.next()
                S.add("pool", lambda e, sl=sl, h=h: e.dma_start(out=sl, in_=wqB_d[h]), writes=[sk], dma=True)
                b = banks.next()
                for k in range(DC):
                    S.add("pe", lambda e, k=k, b=b, sl=sl, W=W: e.matmul(bank(b, w=W), lhsT=sl[:, k, :], rhs=hT[:, k, 0:W],
                                                                       start=(k == 0), stop=(k == DC - 1)),
                          reads=[("hT", k), sk], writes=[bk(b)])
                hs = h % 2
                rs_, nb_, t1_, t2_ = (fm_rstd, nrm_bf, rp_t1, rp_t2) if hs == 0 else scr2
                fm_rmsnorm([b], 128, W, G_GQ, fm_tmp, fm_sq, rs_, [nb_[:, 0:W]], [("nrm_bf", hs)], banks, coff=h % 4, rkey=("fm_rstd", hs))
                rope_fm(nb_[:, 0:W], ("nrm_bf", hs), P, W, tabs[tset][2][:, 0:W], tabs[tset][3][:, 0:W], [("tab", tset, 2), ("tab", tset, 3)],
                        t1_, t2_, qBT[:, h, q0:q0 + W], ("qBT", h, qi), banks, tkey=hs)
        S.barrier()
        A.release(m_p1)

        stop_here("p2a", [("d_qBT", qBT.rearrange("p a b -> p (a b)")), ("d_ckvT", ckvT.rearrange("p a b -> p (a b)")), ("d_krT", krT),
                          ("d_kBT", kBT.rearrange("p a b -> p (a b)")), ("d_vB", vB.rearrange("p a b -> p (a b)"))] if dbg else [])
        cast_a = [(gsl_b[f], gsl_d[f].rearrange("p a b -> p (a b)")) for f in range(DC)]
        cast_a += [(wout_b[c], wout_d[c].rearrange("p a b -> p (a b)")) for c in range(8)]
        cast_a += [(wxq_b[c], wxq_d[c].rearrange("p a b -> p (a b)")) for c in range(2)]
        cast_a += [(wxo_b[c], wxo_d[c].rearrange("p a b -> p (a b)")) for c in range(4)]
        cast_b = [(wup_b[j], wup_d[j].rearrange("p a b -> p (a b)")) for j in range(44)]
        cast_b += [(wdn_b[c * 4 + s4], wdn_d[c, s4].rearrange("p a b -> p (a b)")) for c in range(4) for s4 in range(4)]

        def issue_casts(lst):
            for (dst, src) in lst:
                S.add("pool", lambda e, dst=dst, src=src: e.dma_start(out=dst, in_=src), dma=True)

        if skip_to is None:
            issue_casts(cast_a)

        pts_l = [A([P, 512], BF16) for _ in range(4)]
        rc = A([P, 512], F32)
        stg = [A([P, 512], BF16) for _ in range(2)]
        stg_rot = Rot([(stg[i], ("stg", i)) for i in range(2)])
        pts = Rot([(pts_l[i], ("pt", i)) for i in range(4)])
        wbanks = Rot(range(4))
        acc_rot = Rot([(4, 6), (5, 7)])
        pipe = AttnPipe()
        for h in range(8):
            g = h // 4
            for qi, (q0, W) in enumerate(QBLOCKS):
                ao, as_ = acc_rot.next()

                def finish(h=h, q0=q0, W=W, ao=ao):
                    sg, sgk = stg_rot.next()
                    S.add("dve", lambda e: e.tensor_tensor(out=sg[:, 0:W], in0=bank(ao, w=W), in1=rc[:, 0:W], op=ALU.mult),
                          reads=[bk(ao), "rc"], writes=[sgk])
                    S.add("sp", lambda e: e.dma_start(out=yB_d[h, :, q0:q0 + W], in_=sg[:, 0:W]), reads=[sgk],
                          writes=[("yB_d", h, q0)], dma=True)

                attention_tasks(
                    pipe, 32, W,
                    K1=lambda kt, g=g: kBT[:, g, kt * 128:(kt + 1) * 128], K2=None,
                    Q1=qBT[:, h, q0:q0 + W], Q2=None,
                    Vfn=lambda kt, g=g: vB[:, kt, g * 128:(g + 1) * 128], scale=float(128 ** -0.5),
                    wbanks=wbanks, pts=pts, acc_o=ao, acc_s=as_, rc=rc, finish=finish,
                    rk={"k": [], "q": [], "v": []})
        pipe.run()
        S.barrier()
        A.release(m_gqa)

        stop_here("p3")
        cqT = A([P, 4, NQH + 7], BF16)
        m_p2b = A.mark()
        winq = A([P, DC, 512], BF16)
        xt = [A([P, D], F32) for _ in range(2)]
        xt_rot = Rot([(xt[i], ("xt", i)) for i in range(2)])
        hn = A([P, 4, D], BF16)
        hT = A([P, DC, 512], BF16)
        ss = A([P, 16], F32)
        rt = A([P, 16], F32)
        fm_tmp = A([P, 4, 512], F32)
        fm_sq = A([P, 4, 512], BF16)
        fm_rstd = A([P, 512], F32)
        banks = Rot(range(8))
        for q4 in range(4):
            S.add("pool", lambda e, q4=q4: e.dma_start(out=winq[:, q4 * 4:(q4 + 1) * 4, :], in_=w_in_v[:, q4 * 4:(q4 + 1) * 4, 0:512]),
                  writes=[("winq", q4)], dma=True)
        for qi, (q0, W) in enumerate(QBLOCKS):
            xts = load_x_tiles(xt_rot, q0, W)
            tm_norm_T(xts, hn, ss, rt, G_MIX, hT, "hT", banks, W, std_layout(W))
            cb = [banks.next() for _ in range(4)]
            for cc in range(4):
                for k in range(DC):
                    S.add("pe", lambda e, cc=cc, k=k, b=cb[cc], W=W: e.matmul(
                        bank(b, w=W), lhsT=winq[:, k, cc * 128:(cc + 1) * 128], rhs=hT[:, k, 0:W],
                        start=(k == 0), stop=(k == DC - 1)), reads=[("hT", k), ("winq", k // 4)], writes=[bk(cb[cc])])
            fm_rmsnorm(cb, 512, W, G_MLAQ, fm_tmp, fm_sq, fm_rstd,
                       [cqT[:, cc, q0:q0 + W] for cc in range(4)], [("cqT", cc, qi) for cc in range(4)], banks)
        S.barrier()
        A.release(m_p2b)

        stop_here("p2b", [("d_cqT", cqT.rearrange("p a b -> p (a b)"))] if dbg else [])
        if skip_to == "p4":
            S.stopped = False
            S.add("dve", lambda e: e.memset(ckvT, 0.5), writes=["x1"])
            S.add("dve", lambda e: e.memset(krT, 0.25), writes=["x2"])
            S.add("dve", lambda e: e.memset(cqT, 0.125), writes=["x3"])
            S.barrier()
        Kn = [A([P, SEQ], BF16) for _ in range(2)]
        Vh = [A([P, 32, 128], BF16) for _ in range(2)]
        qn = [A([P, NQH + 7], BF16) for _ in range(2)]
        qr = [A([64, NQH + 7], BF16) for _ in range(2)]
        wq_h = [A([P, 4, 192], BF16) for _ in range(2)]
        wkv_h = [A([P, 4, 256], BF16) for _ in range(2)]
        qtab = [[A([64, 512], F32) for _ in range(2)] for _ in range(2)]
        qr_tmp = A([64, 512], BF16)
        rp_t1 = A([P, 512], F32)
        rp_t2 = A([P, 512], F32)
        pts_l = [A([P, 512], BF16) for _ in range(4)]
        rc = A([P, 512], F32)
        stg = [A([P, 512], BF16) for _ in range(2)]
        stg_rot = Rot([(stg[i], ("stg", i)) for i in range(2)])
        pts = Rot([(pts_l[i], ("pt", i)) for i in range(4)])
        wbanks = Rot(range(4))
        acc_rot = Rot([(4, 6), (5, 7)])
        pipe = AttnPipe()
        qt_i = {"i": 0}

        def mla_head_prep(h):
            hb = h % 2
            S.add("pool", lambda e: e.dma_start(out=wq_h[hb], in_=w_uq_v[:, :, h * 192:(h + 1) * 192]), writes=[("wq_h", hb)], dma=True)
            S.add("pool", lambda e: e.dma_start(out=wkv_h[hb], in_=w_ukv_v[:, :, h * 256:(h + 1) * 256]), writes=[("wkv_h", hb)], dma=True)
            if h + 1 < 8:
                pass
            issue_casts(cast_b[h * 8:(h + 1) * 8] if h < 7 else cast_b[56:])
            for kb in range(8):
                b = wbanks.next()
                for kc in range(4):
                    S.add("pe", lambda e, kc=kc, b=b, kb=kb: e.matmul(
                        bank(b), lhsT=wkv_h[hb][:, kc, 0:128], rhs=ckvT[:, kc, kb * 512:(kb + 1) * 512],
                        start=(kc == 0), stop=(kc == 3)), reads=[("wkv_h", hb)], writes=[bk(b)])
                copy_op(evac_engine(), Kn[hb][:, kb * 512:(kb + 1) * 512], bank(b), [bk(b)], [("Kn", hb)])
            for g4 in range(8):
                b = wbanks.next()
                for j in range(4):
                    kt = g4 * 4 + j
                    for kc in range(4):
                        S.add("pe", lambda e, kc=kc, b=b, kt=kt, j=j: e.matmul(
                            bank(b, w=128, off=j * 128), lhsT=ckvT[:, kc, kt * 128:(kt + 1) * 128], rhs=wkv_h[hb][:, kc, 128:256],
                            start=(kc == 0), stop=(kc == 3)), reads=[("wkv_h", hb)], writes=[bk(b)])
                copy_op(evac_engine(), Vh[hb][:, g4 * 4:(g4 + 1) * 4, :].rearrange("p a b -> p (a b)"), bank(b), [bk(b)], [("Vh", hb)])
            for qi, (q0, W) in enumerate(QBLOCKS):
                ts_ = qt_i["i"] % 2
                qt_i["i"] += 1
                S.add("sp", lambda e, ts_=ts_, q0=q0, W=W: e.dma_start(out=qtab[ts_][0][:, 0:W], in_=rope1[0, :, q0:q0 + W]),
                      writes=[("qtab", ts_, 0)], dma=True)
                S.add("sp", lambda e, ts_=ts_, q0=q0, W=W: e.dma_start(out=qtab[ts_][1][:, 0:W], in_=rope1[1, :, q0:q0 + W]),
                      writes=[("qtab", ts_, 1)], dma=True)
                b = wbanks.next()
                for kc in range(4):
                    S.add("pe", lambda e, kc=kc, b=b, q0=q0, W=W: e.matmul(
                        bank(b, w=W), lhsT=wq_h[hb][:, kc, 0:128], rhs=cqT[:, kc, q0:q0 + W],
                        start=(kc == 0), stop=(kc == 3)), reads=[("wq_h", hb)], writes=[bk(b)])
                copy_op(evac_engine(), qn[hb][:, q0:q0 + W], bank(b, w=W), [bk(b)], [("qn", hb)])
                b2 = wbanks.next()
                for kc in range(4):
                    S.add("pe", lambda e, kc=kc, b2=b2, q0=q0, W=W: e.matmul(
                        bank(b2, parts=64, w=W), lhsT=wq_h[hb][:, kc, 128:192], rhs=cqT[:, kc, q0:q0 + W],
                        start=(kc == 0), stop=(kc == 3)), reads=[("wq_h", hb)], writes=[bk(b2)])
                S.add("act", lambda e, b2=b2, W=W: e.copy(out=qr_tmp[:, 0:W], in_=bank(b2, parts=64, w=W)), reads=[bk(b2)], writes=["qr_tmp"])
                rope_fm(qr_tmp[:, 0:W], "qr_tmp", 64, W, qtab[ts_][0][:, 0:W], qtab[ts_][1][:, 0:W], [("qtab", ts_, 0), ("qtab", ts_, 1)],
                        rp_t1, rp_t2, qr[hb][:, q0:q0 + W], ("qr", hb), wbanks)

        if p4_limit == "prep":
            mla_head_prep(0)
        for h in range(8):
            hb = h % 2
            if p4_limit == "prep" or (p4_limit is not None and p4_limit.startswith("h0") and h > 0):
                break
            if p4_limit is not None and p4_limit.startswith("n") and h >= int(p4_limit[1:]):
                break
            for qi, (q0, W) in enumerate(QBLOCKS):
                if p4_limit == "h0q0" and qi > 0:
                    break
                if p4_limit == "h0q3" and qi > 3:
                    break
                ao, as_ = acc_rot.next()

                def finish(h=h, q0=q0, W=W, ao=ao):
                    sg, sgk = stg_rot.next()
                    S.add("dve", lambda e: e.tensor_tensor(out=sg[:, 0:W], in0=bank(ao, w=W), in1=rc[:, 0:W], op=ALU.mult),
                          reads=[bk(ao), "rc"], writes=[sgk])
                    S.add("sp", lambda e: e.dma_start(out=yA_d[h, :, q0:q0 + W], in_=sg[:, 0:W]), reads=[sgk],
                          writes=[("yA_d", h, q0)], dma=True)

                attention_tasks(
                    pipe, 32, W,
                    K1=lambda kt, hb=hb: Kn[hb][:, kt * 128:(kt + 1) * 128],
                    K2=lambda kt: krT[:, kt * 128:(kt + 1) * 128],
                    Q1=qn[hb][:, q0:q0 + W], Q2=qr[hb][:, q0:q0 + W],
                    Vfn=lambda kt, hb=hb: Vh[hb][:, kt, :], scale=float(192 ** -0.5),
                    wbanks=wbanks, pts=pts, acc_o=ao, acc_s=as_, rc=rc, finish=finish,
                    rk={"k": [("Kn", hb)], "q": [("qn", hb), ("qr", hb)], "v": [("Vh", hb)]},
                    pre=(lambda h=h: mla_head_prep(h)) if qi == 0 else None)
            pipe.run()
            pipe = AttnPipe()
        S.barrier()
        A.release(m_att)

        stop_here("p4")
        KxT = A([P, 4, 256], BF16)
        Vx = A([P, 2, 512], BF16)
        xres = A([P, 4, D], F32)
        hn = A([P, 4, D], BF16)
        hT = A([P, DC, 512], BF16)
        mT = A([P, DC, 512], BF16)
        yAs = A([P, 8, 512], BF16)
        yBs = A([P, 8, 512], BF16)
        ss = A([P, 16], F32)
        rt = A([P, 16], F32)
        gslab = [A([P, 48, 128], BF16) for _ in range(2)]
        gs_rot = Rot([(gslab[i], ("gslab", i)) for i in range(2)])
        wslab = [A([P, DC, 256], BF16) for _ in range(3)]
        ws_rot = Rot([(wslab[i], ("wslab", i)) for i in range(3)])
        oslab = [A([P, 4, 512], BF16) for _ in range(2)]
        os_rot = Rot([(oslab[i], ("oslab", i)) for i in range(2)])
        gA = A([P, 512], F32)
        gB = A([P, 512], F32)
        gt1 = A([P, 512], F32)
        gt2 = A([P, 512], F32)
        qxT = A([P, 4, 512], BF16)
        oxT = A([P, 4, 512], BF16)
        pts_l = [A([P, 512], BF16) for _ in range(4)]
        pts = Rot([(pts_l[i], ("pt", i)) for i in range(4)])
        rc = A([P, 512], F32)
        banks = Rot(range(8))

        mts = []
        for t in range(2):
            S.add("sp", lambda e, t=t: e.dma_start(out=xres[:, t, :], in_=mem[t * 128:(t + 1) * 128, :]), writes=[("xres", t)], dma=True)
            mts.append((lambda t=t: xres[:, t, :], ("xres", t), P))
        tm_norm_T(mts, hn, ss, rt, G_MEM, hT, "hT", banks, 256, [(0, 0, P, 0), (1, 0, P, 128)])
        for half in range(2):
            for q2 in range(2):
                sl, sk = ws_rot.next()
                S.add("pool", lambda e, sl=sl, half=half, q2=q2: e.dma_start(out=sl, in_=wxkv_d[half * 2 + q2]), writes=[sk], dma=True)
                hh = half * 2 + q2
                b = banks.next()
                for k in range(DC):
                    S.add("pe", lambda e, k=k, b=b, sl=sl: e.matmul(bank(b, w=256), lhsT=sl[:, k, 0:128], rhs=hT[:, k, 0:256],
                                                                 start=(k == 0), stop=(k == DC - 1)), reads=[("hT", k), sk], writes=[bk(b)])
                copy_op(evac_engine(), KxT[:, hh, :], bank(b, w=256), [bk(b)], [("KxT", hh)])
                b = banks.next()
                for t in range(2):
                    for k in range(DC):
                        S.add("pe", lambda e, k=k, b=b, sl=sl, t=t: e.matmul(
                            bank(b, w=128, off=t * 128), lhsT=hT[:, k, t * 128:(t + 1) * 128], rhs=sl[:, k, 128:256],
                            start=(k == 0), stop=(k == DC - 1)), reads=[("hT", k), sk], writes=[bk(b)])
                for t in range(2):
                    copy_op(evac_engine(), Vx[:, t, hh * 128:(hh + 1) * 128], bank(b, w=128, off=t * 128), [bk(b)], [("Vx", hh)])

        for qi, (q0, W) in enumerate(QBLOCKS):
            tl = _tiles_of(W)
            xts = []
            for i, (t0, npp) in enumerate(tl):
                S.add("sp", lambda e, i=i, npp=npp, r=q0 + t0: e.dma_start(out=xres[0:npp, i, :], in_=xkv[r:r + npp, :]),
                      writes=[("xres", i)], dma=True)
                xts.append((lambda i=i, npp=npp: xres[0:npp, i, :], ("xres", i), npp))
            S.add("sp", lambda e, q0=q0, W=W: e.dma_start(out=yAs[:, :, 0:W], in_=yA_d[:, :, q0:q0 + W].rearrange("h p w -> p h w")),
                  reads=[("yA_d", h, q0) for h in range(8)], writes=["yAs"], dma=True)
            S.add("sp", lambda e, q0=q0, W=W: e.dma_start(out=yBs[:, :, 0:W], in_=yB_d[:, :, q0:q0 + W].rearrange("h p w -> p h w")),
                  reads=[("yB_d", h, q0) for h in range(8)], writes=["yBs"], dma=True)
            tm_norm_T(xts, hn, ss, rt, G_MIX, hT, "hT", banks, W, std_layout(W))
            for f in range(DC):
                sl, sk = gs_rot.next()
                S.add("pool", lambda e, sl=sl, f=f: e.dma_start(out=sl, in_=gsl_b[f].rearrange("p (a b) -> p a b", b=128)), writes=[sk], dma=True)
                bGA, bGB, bYA, bYB = [banks.next() for _ in range(4)]
                for k in range(DC):
                    S.add("pe", lambda e, k=k, b=bGA, sl=sl, W=W: e.matmul(bank(b, w=W), lhsT=sl[:, k, :], rhs=hT[:, k, 0:W],
                                                                        start=(k == 0), stop=(k == DC - 1)), reads=[("hT", k), sk], writes=[bk(bGA)])
                for k in range(DC):
                    S.add("pe", lambda e, k=k, b=bGB, sl=sl, W=W: e.matmul(bank(b, w=W), lhsT=sl[:, 16 + k, :], rhs=hT[:, k, 0:W],
                                                                        start=(k == 0), stop=(k == DC - 1)), reads=[("hT", k), sk], writes=[bk(bGB)])
                for hh in range(8):
                    S.add("pe", lambda e, hh=hh, b=bYA, sl=sl, W=W: e.matmul(bank(b, w=W), lhsT=sl[:, 32 + hh, :], rhs=yAs[:, hh, 0:W],
                                                                          start=(hh == 0), stop=(hh == 7)), reads=["yAs", sk], writes=[bk(bYA)])
                for hh in range(8):
                    S.add("pe", lambda e, hh=hh, b=bYB, sl=sl, W=W: e.matmul(bank(b, w=W), lhsT=sl[:, 40 + hh, :], rhs=yBs[:, hh, 0:W],
                                                                          start=(hh == 0), stop=(hh == 7)), reads=["yBs", sk], writes=[bk(bYB)])
                S.add("act", lambda e, b=bGA, f=f, W=W: e.activation(out=gA[:, 0:W], in_=bank(b, w=W), func=AF.Sigmoid, bias=spc(BGATE + f)),
                      reads=[bk(bGA), "smallp"], writes=["gA"])
                S.add("act", lambda e, b=bGB, f=f, W=W: e.activation(out=gB[:, 0:W], in_=bank(b, w=W), func=AF.Sigmoid, bias=spc(BGATE + 16 + f)),
                      reads=[bk(bGB), "smallp"], writes=["gB"])
                S.add("dve", lambda e, b=bYA, W=W: e.tensor_tensor(out=gt1[:, 0:W], in0=bank(b, w=W), in1=gA[:, 0:W], op=ALU.mult),
                      reads=[bk(bYA), "gA"], writes=["gt1"])
                S.add("dve", lambda e, b=bYB, W=W: e.tensor_tensor(out=gt2[:, 0:W], in0=bank(b, w=W), in1=gB[:, 0:W], op=ALU.mult),
                      reads=[bk(bYB), "gB"], writes=["gt2"])
                S.add("dve", lambda e, f=f, W=W: e.tensor_tensor(out=mT[:, f, 0:W], in0=gt1[:, 0:W], in1=gt2[:, 0:W], op=ALU.add),
                      reads=["gt1", "gt2"], writes=[("mT", f)])
            for cg in range(8):
                sl, sk = ws_rot.next()
                S.add("pool", lambda e, sl=sl, cg=cg: e.dma_start(out=sl, in_=wout_b[cg].rearrange("p (a b) -> p a b", b=256)), writes=[sk], dma=True)
                for i, (t0, npp) in enumerate(tl):
                    b = banks.next()
                    for k in range(DC):
                        S.add("pe", lambda e, k=k, b=b, sl=sl, t0=t0, npp=npp: e.matmul(
                            bank(b, parts=npp, w=256), lhsT=mT[:, k, t0:t0 + npp], rhs=sl[:, k, :],
                            start=(k == 0), stop=(k == DC - 1)), reads=[("mT", k), sk], writes=[bk(b)])
                    S.add("dve", lambda e, b=b, i=i, npp=npp, cg=cg: e.tensor_tensor(
                        out=xres[0:npp, i, cg * 256:(cg + 1) * 256], in0=bank(b, parts=npp, w=256),
                        in1=xres[0:npp, i, cg * 256:(cg + 1) * 256], op=ALU.add), reads=[bk(b), ("xres", i)], writes=[("xres", i)])
            tm_norm_T(xts, hn, ss, rt, G_CROSS, hT, "hT", banks, W, std_layout(W))
            for c2 in range(2):
                sl, sk = ws_rot.next()
                S.add("pool", lambda e, sl=sl, c2=c2: e.dma_start(out=sl, in_=wxq_b[c2].rearrange("p (a b) -> p a b", b=256)), writes=[sk], dma=True)
                for c1 in range(2):
                    cc = c2 * 2 + c1
                    b = banks.next()
                    for k in range(DC):
                        S.add("pe", lambda e, k=k, b=b, sl=sl, c1=c1, W=W: e.matmul(
                            bank(b, w=W), lhsT=sl[:, k, c1 * 128:(c1 + 1) * 128], rhs=hT[:, k, 0:W],
                            start=(k == 0), stop=(k == DC - 1)), reads=[("hT", k), sk], writes=[bk(b)])
                    copy_op(evac_engine(), qxT[:, cc, 0:W], bank(b, w=W), [bk(b)], [("qxT", cc)])
            pipe = AttnPipe()
            xw = Rot([0, 1, 2, 3])
            xacc = Rot([(4, 6), (5, 7)])
            for hh in range(4):
                ao, as_ = xacc.next()

                def finish(hh=hh, W=W, ao=ao):
                    S.add("dve", lambda e: e.tensor_tensor(out=oxT[:, hh, 0:W], in0=bank(ao, w=W), in1=rc[:, 0:W], op=ALU.mult),
                          reads=[bk(ao), "rc"], writes=[("oxT", hh)])

                attention_tasks(
                    pipe, 2, W,
                    K1=lambda kt, hh=hh: KxT[:, hh, kt * 128:(kt + 1) * 128], K2=None,
                    Q1=qxT[:, hh, 0:W], Q2=None,
                    Vfn=lambda kt, hh=hh: Vx[:, kt, hh * 128:(hh + 1) * 128], scale=float(128 ** -0.5),
                    wbanks=xw, pts=pts, acc_o=ao, acc_s=as_, rc=rc, finish=finish,
                    rk={"k": [("KxT", hh)], "q": [("qxT", hh)], "v": [("Vx", hh)]})
            pipe.run()
            banks = Rot(range(8))
            for cg in range(4):
                sl, sk = os_rot.next()
                S.add("pool", lambda e, sl=sl, cg=cg: e.dma_start(out=sl, in_=wxo_b[cg].rearrange("p (a b) -> p a b", b=512)), writes=[sk], dma=True)
                for i, (t0, npp) in enumerate(tl):
                    b = banks.next()
                    for k in range(4):
                        S.add("pe", lambda e, k=k, b=b, sl=sl, t0=t0, npp=npp: e.matmul(
                            bank(b, parts=npp), lhsT=oxT[:, k, t0:t0 + npp], rhs=sl[:, k, :],
                            start=(k == 0), stop=(k == 3)), reads=[("oxT", k), sk], writes=[bk(b)])
                    S.add("dve", lambda e, b=b, i=i, npp=npp, cg=cg: e.tensor_tensor(
                        out=xres[0:npp, i, cg * 512:(cg + 1) * 512], in0=bank(b, parts=npp),
                        in1=xres[0:npp, i, cg * 512:(cg + 1) * 512], op=ALU.add), reads=[bk(b), ("xres", i)], writes=[("xres", i)])
            for i, (t0, npp) in enumerate(tl):
                S.add("sp", lambda e, i=i, npp=npp, r=q0 + t0: e.dma_start(out=x2_d[r:r + npp, :], in_=xres[0:npp, i, :]),
                      reads=[("xres", i)], writes=["x2_d"], dma=True)
        S.barrier()
        A.release(m_att)

        stop_here("p5a")
        xres = A([P, 4, D], F32)
        nbx = A([33, D], F32)
        hn = A([P, 5, D], BF16)
        hfT = A([P, DC, 512], BF16)
        hfN = A([P, DC, 2], BF16)
        gT = A([P, 44, 512], BF16)
        ss = A([P, 16], F32)
        rt = A([P, 16], F32)
        uslab = [A([P, 32, 128], BF16) for _ in range(3)]
        us_rot = Rot([(uslab[i], ("uslab", i)) for i in range(3)])
        dslab = [A([P, 11, 512], BF16) for _ in range(3)]
        ds_rot = Rot([(dslab[i], ("dslab", i)) for i in range(3)])
        acv = [A([P, 512], F32) for _ in range(2)]
        bcv = [A([P, 512], F32) for _ in range(2)]
        sav = [A([P, 512], F32) for _ in range(2)]
        gfb = A([P, D], F32)
        gm = [A([P, DC, 2], F32) for _ in range(3)]
        S.add("sp", lambda e: e.dma_start(out=gfb, in_=gfin.partition_broadcast(P)), writes=["gfb"], dma=True)
        S.add("dve", lambda e: e.memset(nbx, 0.0), writes=["nbx"])
        for side in range(2):
            S.add("dve", lambda e, side=side: e.tensor_copy(out=gm[1][:, :, side], in_=sp_[:, G_FFN:G_FFN + DC]), reads=["smallp"], writes=["gm1"])
        S.add("dve", lambda e: e.tensor_scalar_mul(out=gm[0][:, :, 0], in0=sp_[:, G_FFN:G_FFN + DC], scalar1=spc(MASKL)), reads=["smallp"], writes=["gm0"])
        S.add("dve", lambda e: e.tensor_copy(out=gm[0][:, :, 1], in_=sp_[:, G_FFN:G_FFN + DC]), reads=["smallp"], writes=["gm0"])
        S.add("dve", lambda e: e.tensor_copy(out=gm[2][:, :, 0], in_=sp_[:, G_FFN:G_FFN + DC]), reads=["smallp"], writes=["gm2"])
        S.add("dve", lambda e: e.tensor_scalar_mul(out=gm[2][:, :, 1], in0=sp_[:, G_FFN:G_FFN + DC], scalar1=spc(MASKR)), reads=["smallp"], writes=["gm2"])
        banks = Rot(range(8))
        for j in range(4):
            r0 = j * 512
            li = r0 - 1 if j > 0 else NQ
            ri = r0 + 512 if j < 3 else NQ
            gmi = 0 if j == 0 else (2 if j == 3 else 1)
            xts = []
            for i in range(4):
                S.add("sp", lambda e, i=i, r=r0 + i * 128: e.dma_start(out=xres[:, i, :], in_=x2_d[r:r + 128, :]),
                      reads=["x2_d"], writes=[("xres", i)], dma=True)
                xts.append((lambda i=i: xres[:, i, :], ("xres", i), P))
            S.add("sp", lambda e, li=li: e.dma_start(out=nbx[0:1, :], in_=x2_d[li:li + 1, :]), reads=["x2_d"], writes=["nbx"], dma=True)
            S.add("sp", lambda e, ri=ri: e.dma_start(out=nbx[32:33, :], in_=x2_d[ri:ri + 1, :]), reads=["x2_d"], writes=["nbx"], dma=True)
            xts.append((lambda: nbx[0:33, :], "nbx", 33))
            layout = [(i, 0, P, i * 128) for i in range(4)] + [(4, 0, 1, 512), (4, 32, 1, 514)]

            def nb_evac(c, pbf, b, gmi=gmi):
                S.add("dve", lambda e: e.tensor_tensor(out=hfN[:, c, 0:2], in0=pbf[:, 512:516:2], in1=gm[gmi][:, c, 0:2], op=ALU.mult),
                      reads=[bk(b), "gm%d" % gmi], writes=[("hfN", c)])

            tm_norm_T(xts, hn, ss, rt, G_FFN, hfT, "hfT", banks, 512, layout, evacs=nb_evac)
            for jj in range(44):
                sl, sk = us_rot.next()
                S.add("pool", lambda e, sl=sl, jj=jj: e.dma_start(out=sl, in_=wup_b[jj].rearrange("p (a b) -> p a b", b=128)), writes=[sk], dma=True)
                ba, bb, bn = banks.next(), banks.next(), banks.next()
                for (bm, so, noff) in ((ba, 0, 0), (bb, 16, 2)):
                    for k in range(DC):
                        S.add("pe", lambda e, k=k, bm=bm, so=so, sl=sl: e.matmul(bank(bm), lhsT=sl[:, so + k, :], rhs=hfT[:, k, :],
                                                                               start=(k == 0), stop=(k == DC - 1)),
                              reads=[("hfT", k), sk], writes=[bk(bm)])
                    for k in range(DC):
                        S.add("pe", lambda e, k=k, so=so, sl=sl, noff=noff: e.matmul(bank(bn, w=2, off=noff), lhsT=sl[:, so + k, :], rhs=hfN[:, k, :],
                                                                                   start=(k == 0), stop=(k == DC - 1)),
                              reads=[("hfN", k), sk], writes=[bk(bn)])
                v = jj % 2
                for (bm, dst, dk, ch, noff) in ((ba, acv[v], ("acv", v), jj, 0), (bb, bcv[v], ("bcv", v), 44 + jj, 2)):
                    w0, w1, w2 = spc(CONVW + ch * 3 + 0), spc(CONVW + ch * 3 + 1), spc(CONVW + ch * 3 + 2)
                    S.add("act", lambda e, bm=bm, dst=dst, ch=ch, w1=w1: e.activation(
                        out=dst, in_=bank(bm), func=AF.Identity, scale=w1, bias=spc(CONVB + ch)),
                        reads=[bk(bm), "smallp"], writes=[dk])
                    S.add("dve", lambda e, bm=bm, dst=dst, w0=w0: e.scalar_tensor_tensor(
                        out=dst[:, 1:512], in0=bank(bm, w=511, off=0), scalar=w0, in1=dst[:, 1:512], op0=ALU.mult, op1=ALU.add),
                        reads=[bk(bm), "smallp", dk], writes=[dk])
                    S.add("dve", lambda e, dst=dst, w0=w0, noff=noff: e.scalar_tensor_tensor(
                        out=dst[:, 0:1], in0=bank(bn, w=1, off=noff), scalar=w0, in1=dst[:, 0:1], op0=ALU.mult, op1=ALU.add),
                        reads=[bk(bn), "smallp", dk], writes=[dk])
                    S.add("dve", lambda e, bm=bm, dst=dst, w2=w2: e.scalar_tensor_tensor(
                        out=dst[:, 0:511], in0=bank(bm, w=511, off=1), scalar=w2, in1=dst[:, 0:511], op0=ALU.mult, op1=ALU.add),
                        reads=[bk(bm), "smallp", dk], writes=[dk])
                    S.add("dve", lambda e, dst=dst, w2=w2, noff=noff: e.scalar_tensor_tensor(
                        out=dst[:, 511:512], in0=bank(bn, w=1, off=noff + 1), scalar=w2, in1=dst[:, 511:512], op0=ALU.mult, op1=ALU.add),
                        reads=[bk(bn), "smallp", dk], writes=[dk])
                S.add("act", lambda e, v=v: e.activation(out=sav[v], in_=acv[v], func=AF.Silu), reads=[("acv", v)], writes=[("sav", v)])
                S.add("dve", lambda e, v=v, jj=jj: e.tensor_tensor(out=gT[:, jj, :], in0=sav[v], in1=bcv[v], op=ALU.mult),
                      reads=[("sav", v), ("bcv", v)], writes=[("gT", jj)])
            for cg in range(4):
                accb = [banks.next() for _ in range(4)]
                for s4 in range(4):
                    sl, sk = ds_rot.next()
                    S.add("pool", lambda e, sl=sl, s4=s4, cg=cg: e.dma_start(out=sl, in_=wdn_b[cg * 4 + s4].rearrange("p (a b) -> p a b", b=512)), writes=[sk], dma=True)
                    for i in range(4):
                        for kk in range(11):
                            kc = s4 * 11 + kk
                            S.add("pe", lambda e, i=i, kk=kk, kc=kc, sl=sl, b=accb[i]: e.matmul(
                                bank(b), lhsT=gT[:, kc, i * 128:(i + 1) * 128], rhs=sl[:, kk, :],
                                start=(kc == 0), stop=(kc == 43)), reads=[("gT", kc), sk], writes=[bk(accb[i])])
                for i in range(4):
                    S.add("dve", lambda e, i=i, b=accb[i], cg=cg: e.tensor_tensor(
                        out=xres[:, i, cg * 512:(cg + 1) * 512], in0=bank(b), in1=xres[:, i, cg * 512:(cg + 1) * 512], op=ALU.add),
                        reads=[bk(accb[i]), ("xres", i)], writes=[("xres", i)])
            if j == 0:
                stop_here("p5b0", [("d_hfT", hfT.rearrange("p a b -> p (a b)")), ("d_gT", gT.rearrange("p a b -> p (a b)")), ("d_acv", acv[1]),
                                   ("d_bcv", bcv[1]), ("d_x3", xres.rearrange("p a b -> p (a b)"))] if dbg else [])
            S.add("dve", lambda e: e.memset(ss[:, 0:8], 0.0), writes=["ss"])
            for i in range(4):
                S.add("act", lambda e, i=i: e.activation(out=hn[:, i, :], in_=xres[:, i, :], func=AF.Square, accum_out=ss[:, i:i + 1]),
                      reads=[("xres", i), "ss"], writes=[("hn", i), ("ssv", i)])
            S.add("act", lambda e: e.activation(out=rt[:, 0:4], in_=ss[:, 0:4], func=AF.Sqrt, scale=1.0 / D, bias=epst),
                  reads=[("ssv", i) for i in range(4)] + ["eps", "ss"], writes=[("rt", 0)])
            S.add("dve", lambda e: e.reciprocal(out=rt[:, 4:8], in_=rt[:, 0:4]), reads=[("rt", 0)], writes=[("rs", 0)])
            for i in range(4):
                S.add("dve", lambda e, i=i: e.scalar_tensor_tensor(out=xres[:, i, :], in0=xres[:, i, :], scalar=rt[:, 4 + i:5 + i], in1=gfb,
                                                                   op0=ALU.mult, op1=ALU.mult), reads=[("xres", i), ("rs", 0), "gfb"], writes=[("xres", i)])
                S.add("sp", lambda e, i=i, r=r0 + i * 128: e.dma_start(out=out[r:r + 128, :], in_=xres[:, i, :]),
                      reads=[("xres", i)], writes=[("out", r0, i)], dma=True)
        S.barrier()
        S.finalize(ctx)
        build_program.peak = A.peak
    return nc


def _rope_tables(pos_list):
    f32 = np.float32
    half = 32
    inv = np.power(f32(10000.0), (-2.0 * np.arange(half, dtype=f32) / f32(64)).astype(f32)).astype(f32)
    ang = (pos_list.astype(f32)[None, :] * inv[:, None]).astype(f32)
    c = np.cos(ang).astype(f32)
    s = np.sin(ang).astype(f32)
    C = np.concatenate([c, c], axis=0)
    Sg = np.concatenate([-s, s], axis=0)
    return np.stack([C, Sg], axis=0).astype(f32)


def _prep_core(core, inputs, shared):
    b, half = core // 2, core % 2
    own = np.arange(half * NQ, half * NQ + NQ)
    if half == 0:
        other = np.arange(NQ, SEQ)
    else:
        other = np.concatenate([[NQ - 1], np.arange(0, NQ - 1)])
    order = np.concatenate([own, other])
    x = inputs["x"]
    m = dict(shared)
    m["xkv"] = np.ascontiguousarray(x[b][order])
    m["mem"] = np.ascontiguousarray(inputs["mem"][b])
    tpos = order.astype(np.float32)
    rows = (order // 64).astype(np.float32)
    cols = (order % 64).astype(np.float32)
    m["rope1"] = _rope_tables(tpos)
    r_rows = _rope_tables(rows)
    r_cols = _rope_tables(cols)
    m["rope2"] = np.ascontiguousarray(np.concatenate([r_rows, r_cols], axis=1))
    sp = shared["smallp"].copy()
    sp[:, 74] = 0.0 if half == 0 else 1.0
    sp[:, 75] = 1.0 if half == 0 else 0.0
    m["smallp"] = sp
    return m


def _shared_inputs(inputs):
    f32 = np.float32
    sh = {}
    for k in ("w_in", "w_uq", "w_ukv"):
        sh[k] = np.ascontiguousarray(inputs[k][0], dtype=f32)

    def kview(w):
        w = np.asarray(w, f32)
        return w.reshape(w.shape[0] // P, P, w.shape[1]).transpose(1, 0, 2)

    wi = kview(inputs["w_in"][0])
    sh["wqB_d"] = np.ascontiguousarray(wi[:, :, 1088:2112].reshape(P, DC, 8, 128).transpose(2, 0, 1, 3))
    wg = kview(inputs["w_gate"][0])
    gsl = np.empty((DC, P, 48, 128), f32)
    gsl[:, :, 0:16] = wg[:, :, 0:D].reshape(P, DC, DC, 128).transpose(2, 0, 1, 3)
    gsl[:, :, 16:32] = wg[:, :, D:2 * D].reshape(P, DC, DC, 128).transpose(2, 0, 1, 3)
    gsl[:, :, 32:40] = kview(inputs["w_o_mla"][0]).reshape(P, 8, DC, 128).transpose(2, 0, 1, 3)
    gsl[:, :, 40:48] = kview(inputs["w_o_gqa"][0]).reshape(P, 8, DC, 128).transpose(2, 0, 1, 3)
    sh["gsl_d"] = gsl
    sh["wout_d"] = np.ascontiguousarray(kview(inputs["w_out"][0]).reshape(P, DC, 8, 256).transpose(2, 0, 1, 3))
    sh["wxq_d"] = np.ascontiguousarray(kview(inputs["w_xq"][0]).reshape(P, DC, 2, 256).transpose(2, 0, 1, 3))
    sh["wxkv_d"] = np.ascontiguousarray(kview(inputs["w_xkv"][0]).reshape(P, DC, 4, 256).transpose(2, 0, 1, 3))
    sh["wxo_d"] = np.ascontiguousarray(kview(inputs["w_xo"][0]).reshape(P, 4, 4, 512).transpose(2, 0, 1, 3))
    sh["wup_d"] = np.ascontiguousarray(
        kview(inputs["w_up"][0]).reshape(P, DC, 2, 44, 128).transpose(3, 0, 2, 1, 4).reshape(44, P, 32, 128))
    sh["wdn_d"] = np.ascontiguousarray(kview(inputs["w_down"][0]).reshape(P, 4, 11, 4, 512).transpose(3, 1, 0, 2, 4))
    sp = np.zeros((P, 512), f32)

    def cols(v, n):
        return np.asarray(v, f32).reshape(n, P).T

    sp[:, 0:16] = cols(inputs["norm_mix"][0], 16)
    sp[:, 16:32] = cols(inputs["norm_cross"][0], 16)
    sp[:, 32:48] = cols(inputs["norm_mem"][0], 16)
    sp[:, 48:64] = cols(inputs["norm_ffn"][0], 16)
    sp[:, 64:68] = cols(inputs["mla_q_norm"][0], 4)
    sp[:, 68:72] = cols(inputs["mla_kv_norm"][0], 4)
    sp[:, 72:73] = cols(inputs["gqa_q_norm"][0], 1)
    sp[:, 73:74] = cols(inputs["gqa_k_norm"][0], 1)
    sp[:, 80:112] = cols(inputs["b_gate"][0], 32)
    sp[:, 112:200] = cols(inputs["conv_b"][0], 88)
    cw = np.asarray(inputs["conv_w"][0], f32)
    sp[:, 200:464] = cw.reshape(3, 88, P).transpose(2, 1, 0).reshape(P, 264)
    sh["smallp"] = sp
    sh["gfin"] = np.asarray(inputs["norm_final"], f32).reshape(1, D)
    sh["ident"] = np.eye(P, dtype=f32)
    pm = np.zeros((P, P), f32)
    for f in range(P):
        blk, r = f // 64, f % 64
        pm[blk * 64 + (r + 32) % 64, f] = 1.0
    sh["perm"] = pm
    return sh


_CACHE = {}


def kernel(**inputs):
    inputs = {k: np.asarray(v) for k, v in inputs.items()}
    if "nc" not in _CACHE:
        _CACHE["nc"] = build_program()
    nc = _CACHE["nc"]
    shared = _shared_inputs(inputs)
    in_maps = [_prep_core(c, inputs, shared) for c in range(8)]
    res = run_bass_kernel_spmd(nc, in_maps, core_ids=list(range(8)))
    outp = np.empty((4, SEQ, D), np.float32)
    for c in range(8):
        b, half = c // 2, c % 2
        outp[b, half * NQ:(half + 1) * NQ, :] = res.results[c]["out"]
    return outp
```
